# Optimizing a Trainium2 kernel written in Bass

```python
import jax, jax.numpy as jnp
from jax import lax
import numpy as np

D_MODEL = 1024
BATCH = 2
SEQ = 8192
DEPTH = 1

CONV_WIDTH = D_MODEL
CONV_K = 3
RET_HEADS = 4
RET_QK_DIM = D_MODEL // RET_HEADS
RET_V_DIM = 2 * RET_QK_DIM
RET_QK = RET_HEADS * RET_QK_DIM
RET_V = RET_HEADS * RET_V_DIM
CHUNK = 128
ROPE_BASE = 10000.0
D_FF = -(-8 * D_MODEL // (3 * 256)) * 256
EPS = 1e-6
IN_SIZES = (CONV_WIDTH, CONV_WIDTH, CONV_WIDTH, RET_QK, RET_QK, RET_V, RET_V, D_MODEL, D_MODEL)
N_IN = sum(IN_SIZES)

kernel_name = "hybrid_shortconv_retention_gated_block"


def rmsnorm(x, g):
    xf = x.astype(jnp.float32)
    y = xf * lax.rsqrt(jnp.mean(xf * xf, axis=-1, keepdims=True) + EPS)
    return (y * g.astype(jnp.float32)).astype(x.dtype)


def rotary(t):
    s, d = t.shape[1], t.shape[-1]
    pos = jnp.arange(s, dtype=jnp.float32)
    freqs = ROPE_BASE ** (-jnp.arange(0, d, 2, dtype=jnp.float32) / d)
    ang = pos[:, None] * freqs[None, :]
    cos = jnp.cos(ang)[None, :, None, :]
    sin = jnp.sin(ang)[None, :, None, :]
    t1, t2 = t[..., : d // 2], t[..., d // 2:]
    return jnp.concatenate([t1 * cos - t2 * sin, t1 * sin + t2 * cos], axis=-1)


def retention_chunkwise(q, k, v, log_gamma, strict):
    b, h, s, dk = q.shape
    dv = v.shape[-1]
    n_chunks = s // CHUNK

    def chunks(t):
        return jnp.moveaxis(t.reshape(b, h, n_chunks, CHUNK, t.shape[-1]), 2, 0)

    idx = jnp.arange(CHUNK, dtype=jnp.float32)
    diff = idx[:, None] - idx[None, :]
    mask = (diff > 0) if strict else (diff >= 0)
    lg = log_gamma[:, None, None]
    intra_decay = jnp.where(mask[None], jnp.exp(jnp.maximum(diff, 0.0)[None] * lg), 0.0)
    xi = jnp.exp((idx + 1.0)[None, :] * log_gamma[:, None])[..., None]
    zeta = jnp.exp((CHUNK - 1.0 - idx)[None, :] * log_gamma[:, None])[..., None]
    chunk_decay = jnp.exp(CHUNK * log_gamma)[:, None, None]

    def step(state, qkv):
        qc, kc, vc = qkv
        scores = jnp.einsum('bhcd,bhmd->bhcm', qc, kc) * intra_decay
        inner = jnp.einsum('bhcm,bhme->bhce', scores, vc)
        cross = jnp.einsum('bhcd,bhde->bhce', qc, state) * xi
        state = chunk_decay * state + jnp.einsum('bhcd,bhce->bhde', kc * zeta, vc)
        return state, inner + cross

    state0 = jnp.zeros((b, h, dk, dv), jnp.float32)
    _, out = lax.scan(step, state0, (chunks(q), chunks(k), chunks(v)))
    return jnp.moveaxis(out, 0, 2).reshape(b, h, s, dv)


def setup_inputs(seed: int = 0) -> dict:
    key = jax.random.key(seed)
    ks = jax.random.split(key, 16)
    f32 = jnp.float32

    def normal(k, shape, scale):
        return jax.random.normal(k, shape, f32) * scale

    def gain(k, shape):
        return 1.0 + 0.05 * jax.random.normal(k, shape, f32)

    base_logit = jnp.log(2.0 ** (5.0 + jnp.arange(RET_HEADS, dtype=f32)) - 1.0)
    return {
        "x": jax.random.normal(ks[0], (BATCH, SEQ, D_MODEL), f32),
        "g_mix": gain(ks[1], (DEPTH, D_MODEL)),
        "w_in": normal(ks[2], (DEPTH, D_MODEL, N_IN), D_MODEL ** -0.5),
        "w_conv": normal(ks[3], (DEPTH, CONV_K, CONV_WIDTH), CONV_K ** -0.5),
        "dec_f": base_logit[None] + 0.01 * jax.random.normal(ks[4], (DEPTH, RET_HEADS), f32),
        "dec_b": base_logit[None] + 0.01 * jax.random.normal(ks[5], (DEPTH, RET_HEADS), f32),
        "g_ret": gain(ks[6], (DEPTH, RET_V)),
        "w_a_out": normal(ks[7], (DEPTH, CONV_WIDTH, D_MODEL), CONV_WIDTH ** -0.5),
        "w_r_out": normal(ks[8], (DEPTH, RET_V, D_MODEL), RET_V ** -0.5),
        "w_o": normal(ks[9], (DEPTH, D_MODEL, D_MODEL), D_MODEL ** -0.5),
        "g_ffn": gain(ks[10], (DEPTH, D_MODEL)),
        "w_ff_gate": normal(ks[11], (DEPTH, D_MODEL, D_FF), D_MODEL ** -0.5),
        "w_ff_up": normal(ks[12], (DEPTH, D_MODEL, D_FF), D_MODEL ** -0.5),
        "w_ff_down": normal(ks[13], (DEPTH, D_FF, D_MODEL), D_FF ** -0.5),
        "g_final": gain(ks[14], (D_MODEL,)),
    }


def reference(x, g_mix, w_in, w_conv, dec_f, dec_b, g_ret, w_a_out, w_r_out, w_o,
              g_ffn, w_ff_gate, w_ff_up, w_ff_down, g_final):
    b, s, _ = x.shape
    dt = x.dtype
    split_points = [int(p) for p in np.cumsum(IN_SIZES)[:-1]]
    for l in range(DEPTH):
        h = rmsnorm(x, g_mix[l])
        proj = h @ w_in[l]
        xc, gb, gc, q, k, v, g_sw, gate_a, gate_r = jnp.split(proj, split_points, axis=-1)

        u = gc * xc
        u_pad = jnp.pad(u, ((0, 0), (CONV_K // 2, CONV_K // 2), (0, 0)))
        conv = u_pad[:, 0:s] * w_conv[l, 0]
        for tap in range(1, CONV_K):
            conv = conv + u_pad[:, tap:tap + s] * w_conv[l, tap]
        y_a = (gb * conv) @ w_a_out[l]

        qf = rotary(q.reshape(b, s, RET_HEADS, RET_QK_DIM).astype(jnp.float32)) * (RET_QK_DIM ** -0.5)
        kf = rotary(k.reshape(b, s, RET_HEADS, RET_QK_DIM).astype(jnp.float32))
        vf = v.reshape(b, s, RET_HEADS, RET_V_DIM).astype(jnp.float32)
        qf, kf, vf = (jnp.transpose(t, (0, 2, 1, 3)) for t in (qf, kf, vf))
        lg_f = jax.nn.log_sigmoid(dec_f[l].astype(jnp.float32))
        lg_b = jax.nn.log_sigmoid(dec_b[l].astype(jnp.float32))
        ret_f = retention_chunkwise(qf, kf, vf, lg_f, strict=False)
        ret_b = jnp.flip(retention_chunkwise(jnp.flip(qf, 2), jnp.flip(kf, 2), jnp.flip(vf, 2),
                                             lg_b, strict=True), 2)
        ret = ret_f + ret_b
        mu = jnp.mean(ret, axis=-1, keepdims=True)
        var = jnp.mean(jnp.square(ret - mu), axis=-1, keepdims=True)
        ret = (ret - mu) * lax.rsqrt(var + EPS)
        ret = jnp.transpose(ret, (0, 2, 1, 3)).reshape(b, s, RET_V) * g_ret[l].astype(jnp.float32)
        y_r = (ret.astype(dt) * jax.nn.silu(g_sw)) @ w_r_out[l]

        merged = jax.nn.sigmoid(gate_a) * y_a + jax.nn.sigmoid(gate_r) * y_r
        x = x + merged @ w_o[l]

        h2 = rmsnorm(x, g_ffn[l])
        x = x + (jax.nn.silu(h2 @ w_ff_gate[l]) * (h2 @ w_ff_up[l])) @ w_ff_down[l]
    return rmsnorm(x, g_final)
```

```python
import math
import contextlib
import numpy as np
import concourse.bass as bass
import concourse.mybir as mybir
from concourse.bass_utils import run_bass_kernel_spmd

F32 = mybir.dt.float32
BF16 = mybir.dt.bfloat16
I32 = mybir.dt.int32
U8 = mybir.dt.uint8
AF = mybir.ActivationFunctionType
OP = mybir.AluOpType

D = 1024
SEQ = 8192
NSEG = 4
T = SEQ // NSEG
NT = T // 128
NG = T // 512
H = 4
DK = 256
DV = 512
DFF = 2816
NFF = DFF // 128
NIN = 11264
EPS = 1e-6
C_XC, C_GB, C_GC, C_Q, C_K, C_V, C_GSW, C_GA, C_GR = 0, 1024, 2048, 3072, 4096, 5120, 7168, 9216, 10240
ARENA_BYTES = 206 * 1024
SAME_ENGINE_SYNC = True
TWO_PI_HI = 6.28125
TWO_PI_LO = 2.0 * math.pi - 6.28125


def _dsize(dt):
    return {F32: 4, BF16: 2, I32: 4, U8: 1}[dt]


class Buf:
    __slots__ = ("name", "w", "r", "sem", "dcnt")

    def __init__(self, name):
        self.name = name
        self.w = {}
        self.r = {}
        self.sem = None
        self.dcnt = 0


class KB:
    def __init__(self, nc):
        self.nc = nc
        self.es = contextlib.ExitStack()
        self.eng = {"pe": nc.tensor, "act": nc.scalar, "dve": nc.vector, "pool": nc.gpsimd, "sp": nc.sync}
        self.sems = {}
        self.cnt = {}
        for e in ("pe", "act", "dve", "pool"):
            self.sems[e] = self.es.enter_context(nc.semaphore("s_" + e))
            self.cnt[e] = 0
        self.seen = {e: {} for e in self.eng}
        self.snap = {}
        self.all = {}
        self.dbufs = {}
        self.nwaits = 0
        self.nops = 0

    def _cur(self, k):
        if k in self.cnt:
            return self.cnt[k]
        return self.dbufs[k].dcnt

    def _emit_waits(self, e, reads, writes, partial, attach=False):
        need = {}
        todo = []

        def add(d):
            for k, v in d.items():
                if need.get(k, 0) < v:
                    need[k] = v

        for b in reads:
            add(b.w)
        for b in writes:
            add(b.w)
            add(b.r)
        for b in partial:
            add(b.r)
        seen = self.seen[e]
        eng = self.eng[e]
        for k, v in need.items():
            if k == e and (e == "pe" or not SAME_ENGINE_SYNC):
                continue
            if k in self.dbufs:
                v = max(v, self.dbufs[k].dcnt)
            if seen.get(k, 0) >= v:
                continue
            assert v <= self._cur(k), ("wait on un-issued event", e, k, v, self._cur(k))
            todo.append((k, v))
            seen[k] = v
            sn = self.snap.get((k, v))
            if sn:
                for k2, v2 in sn.items():
                    if seen.get(k2, 0) < v2:
                        seen[k2] = v2
            self.nwaits += 1
        keep = todo[-1:] if (attach and todo) else []
        for k, v in (todo[:-1] if (attach and todo) else todo):
            eng.wait_ge(self.sems[k], v)
        return keep

    def op(self, e, fn, reads=(), writes=(), partial=(), inc=True, multi=False):
        keep = self._emit_waits(e, reads, writes, partial, attach=not multi)
        ins = fn(self.eng[e])
        for k, v in keep:
            ins._wait_ge(self.sems[k], v)
        self.nops += 1
        if inc:
            self.cnt[e] += 1
            ins.then_inc(self.sems[e], 1)
            val = self.cnt[e]
            self.all[e] = val
            self.snap[(e, val)] = dict(self.seen[e])
        else:
            val = self.cnt[e] + 1
        for b in reads:
            if b.r.get(e, 0) < val:
                b.r[e] = val
        for b in writes:
            b.w = {e: val}
            b.r = {}
        for b in partial:
            if b.w.get(e, 0) < val:
                b.w[e] = val
        return ins

    def dma(self, q, out_ap, in_ap, sb, reads=(), writes=(), partial=(), **kw):
        if sb.sem is None:
            sb.sem = self.es.enter_context(self.nc.semaphore("d_" + sb.name))
            key = "d:" + sb.name
            assert key not in self.sems, key
            self.sems[key] = sb.sem
            self.dbufs[key] = sb
        key = "d:" + sb.name
        keep = self._emit_waits(q, reads, writes, partial, attach=True)
        ins = self.eng[q].dma_start(out=out_ap, in_=in_ap, **kw)
        for k, v in keep:
            ins._wait_ge(self.sems[k], v)
        sb.dcnt += 16
        ins.then_inc(sb.sem, 16)
        val = sb.dcnt
        self.all[key] = val
        self.snap[(key, val)] = dict(self.seen[q])
        self.nops += 1
        for b in reads:
            if b.r.get(key, 0) < val:
                b.r[key] = val
        for b in writes:
            b.w = {key: val}
            b.r = {}
        for b in partial:
            if b.w.get(key, 0) < val:
                b.w[key] = val
        return ins

    def barrier(self, engines=("pe", "act", "dve", "pool", "sp")):
        for e in engines:
            seen = self.seen[e]
            eng = self.eng[e]
            for k, v in self.all.items():
                if k == "cc":
                    continue
                if k == e and (e == "pe" or not SAME_ENGINE_SYNC):
                    continue
                if seen.get(k, 0) >= v:
                    continue
                eng.wait_ge(self.sems[k], v)
                seen[k] = v
                sn = self.snap.get((k, v))
                if sn:
                    for k2, v2 in sn.items():
                        if seen.get(k2, 0) < v2:
                            seen[k2] = v2
                self.nwaits += 1


class Arena:
    def __init__(self, nc, nbytes):
        self.nc = nc
        self.t = nc.alloc_sbuf_tensor("arena", [128, nbytes], U8)
        self.base = nc.lookup_mloc(self.t).addr
        self.nbytes = nbytes
        self.top = 0
        self.n = 0
        self.peak = 0

    def alloc(self, name, shape, dtype):
        size = _dsize(dtype)
        for s in shape[1:]:
            size *= s
        size = (size + 31) // 32 * 32
        off = self.base + self.top
        self.top += size
        self.peak = max(self.peak, self.top)
        assert self.top <= self.nbytes, ("arena overflow", name, self.top)
        self.n += 1
        return self.nc.alloc_sbuf_tensor_at(f"{name}_{self.n}", list(shape), dtype, offset=off)

    def mark(self):
        return self.top

    def release(self, m):
        self.top = m


def build_program(debug=None):
    debug = debug or {}
    stop = debug.get("stop", 99)
    nc = bass.Bass("TRN2", target_bir_lowering=False)
    kb = KB(nc)
    ar = Arena(nc, ARENA_BYTES)
    dumps = []

    def dram_in(name, shape, dt=F32):
        return nc.dram_tensor(name, list(shape), dt, kind="ExternalInput").ap()

    x_seg = dram_in("x_seg", [T, D])
    x_halo = dram_in("x_halo", [2, D])
    pos_in = dram_in("pos", [1, T])
    freq_in = dram_in("freq", [128, 1])
    segw_in = dram_in("segw", [1, 16])
    dec_in = dram_in("dec", [1, 8])
    gmixT_in = dram_in("g_mixT", [128, 8])
    gffnT_in = dram_in("g_ffnT", [128, 8])
    wconvT_in = dram_in("w_convT", [128, 24])
    gret_in = dram_in("g_ret", [1, H * DV])
    gfin_in = dram_in("g_final", [1, D])
    w_in = dram_in("w_in", [D, NIN])
    w_a_out = dram_in("w_a_out", [D, D])
    w_r_out = dram_in("w_r_out", [H * DV, D])
    w_o = dram_in("w_o", [D, D])
    w_gate = dram_in("w_ff_gate", [D, DFF])
    w_up = dram_in("w_ff_up", [D, DFF])
    w_down = dram_in("w_ff_down", [DFF, D])
    y_out = nc.dram_tensor("y", [T, D], F32, kind="ExternalOutput").ap()

    kT_d = nc.dram_tensor("kT_d", [H, 128, 2 * T], BF16, kind="Internal").ap()
    V_d = nc.dram_tensor("V_d", [H, 128, NT * DV], BF16, kind="Internal").ap()
    z_d = nc.dram_tensor("z_d", [128, 8 * T], BF16, kind="Internal").ap()
    gT_d = nc.dram_tensor("gT_d", [H, 128, 4 * T], BF16, kind="Internal").ap()
    x1_d = nc.dram_tensor("x1_d", [T, D], F32, kind="Internal").ap()
    cc_src = [nc.dram_tensor(f"cc_src{h}", [2 * 2 * 128, DV], F32, kind="Internal").ap() for h in range(H)]
    cc_dst = [nc.dram_tensor(f"cc_dst{h}", [NSEG * 2 * 2 * 128, DV], F32, kind="Internal").ap() for h in range(H)]
    sin_d = nc.dram_tensor("sin_d", [H * 2, 128, 2 * DV], F32, kind="Internal").ap()
    b_sin_d = [Buf(f"sin_d{i}") for i in range(H * 2)]
    b_kT_d = [Buf(f"kT_d{h}") for h in range(H)]
    b_V_d = [Buf(f"V_d{h}") for h in range(H)]
    b_z_d = Buf("z_d")
    b_gT_d = [Buf(f"gT_d{h}") for h in range(H)]
    b_x1_d = [Buf(f"x1_d{i}") for i in range(NT)]
    b_cc_src = [Buf(f"cc_src{h}") for h in range(H)]
    b_cc_dst = [Buf(f"cc_dst{h}") for h in range(H)]
    cc_sem = kb.es.enter_context(nc.semaphore("cc"))
    kb.sems["cc"] = cc_sem
    kb.cnt["cc"] = 0
    b_in = Buf("inputs")

    w_in_v = w_in.rearrange("(c p) n -> p c n", p=128)

    ps_all = nc.alloc_psum_tensor("ps_all", [128, 8, 512], F32)
    PB = [Buf(f"pb{i}") for i in range(8)]

    def psf(i, n=512):
        return ps_all[:, i, 0:n]

    def psb(i):
        return ps_all[:, i, :].bitcast(BF16)

    def dump(name, ap, buf, shape, dt):
        t = nc.dram_tensor("dbg_" + name, list(shape), dt, kind="ExternalOutput").ap()
        kb.dma("sp", t, ap, buf, reads=[buf])
        dumps.append("dbg_" + name)

    def finish():
        kb.barrier(engines=("sp",))
        return nc, dumps, kb, ar

    cbuf = Buf("consts")
    ident = ar.alloc("ident", [128, 128], BF16)
    gmixT = ar.alloc("gmixT", [128, 8], F32)
    gffnT = ar.alloc("gffnT", [128, 8], F32)
    wconvT = ar.alloc("wconvT", [128, 24], F32)
    dec = ar.alloc("dec", [128, 8], F32)
    segw = ar.alloc("segw", [128, 16], F32)
    freq = ar.alloc("freq", [128, 1], F32)
    lg = ar.alloc("lg", [128, 8], F32)
    g128 = ar.alloc("g128", [128, 8], F32)
    zcol = ar.alloc("zcol", [128, 8], F32)
    zp = ar.alloc("zp", [128, 8, NT], F32)
    wseg = ar.alloc("wseg", [128, 8, 4], F32)
    maskT = ar.alloc("maskT", [128, H, 128], F32)
    xif = ar.alloc("xif", [128, H, 128], F32)
    xib = ar.alloc("xib", [128, H, 128], F32)
    b_ident, b_lg, b_tabs = Buf("ident"), Buf("lg"), Buf("tabs")
    for dst, src in ((gmixT, gmixT_in), (gffnT, gffnT_in), (wconvT, wconvT_in), (freq, freq_in)):
        kb.dma("sp", dst[:], src, cbuf, reads=[b_in], partial=[cbuf])
    kb.dma("sp", dec[:], dec_in[0:1, :].partition_broadcast(128), cbuf, reads=[b_in], partial=[cbuf])
    kb.dma("sp", segw[:], segw_in[0:1, :].partition_broadcast(128), cbuf, reads=[b_in], partial=[cbuf])

    m_setup = ar.mark()
    it_i = ar.alloc("it_i", [128, 128], I32)
    jmi = ar.alloc("jmi", [128, 128], F32)
    fi1 = ar.alloc("fi1", [128, 128], F32)
    fr = ar.alloc("fr", [128, 128], F32)
    pp1 = ar.alloc("pp1", [128, 1], F32)
    pr = ar.alloc("pr", [128, 1], F32)
    pi_ = ar.alloc("pi_", [128, 1], F32)
    tpos = ar.alloc("tpos", [128, NT], F32)
    rpos = ar.alloc("rpos", [128, NT], F32)
    tmpA = ar.alloc("tmpA", [128, 128], F32)
    tmpB = ar.alloc("tmpB", [128, 128], F32)
    tmpC = ar.alloc("tmpC", [128, 128], F32)
    tmpM = ar.alloc("tmpM", [128, 128], F32)
    s8 = [ar.alloc(f"s8_{i}", [128, 8], F32) for i in range(6)]
    b_it, b_set = Buf("it"), Buf("setup")
    bt = [Buf(f"tmp{i}") for i in range(12)]

    def iota_f32(dst, dstbuf, pattern, base, cm, n):
        kb.op("pool", lambda e: e.iota(it_i[:, 0:n], pattern=pattern, base=base, channel_multiplier=cm), writes=[b_it])
        kb.op("dve", lambda e: e.tensor_copy(out=dst, in_=it_i[:, 0:n]), reads=[b_it], partial=[dstbuf])

    iota_f32(jmi[:], b_set, [[-1, 128]], 0, 1, 128)
    kb.op("dve", lambda e: e.tensor_single_scalar(out=tmpA[:], in_=jmi[:], scalar=0.0, op=OP.is_equal), reads=[b_set], writes=[bt[0]])
    kb.op("dve", lambda e: e.tensor_copy(out=ident[:], in_=tmpA[:]), reads=[bt[0]], writes=[b_ident])
    iota_f32(fi1[:], b_set, [[1, 128]], 1, 0, 128)
    iota_f32(fr[:], b_set, [[-1, 128]], 128, 0, 128)
    iota_f32(pp1[:], b_set, [[0, 1]], 1, 1, 1)
    iota_f32(pr[:], b_set, [[0, 1]], 127, -1, 1)
    iota_f32(pi_[:], b_set, [[0, 1]], 0, 1, 1)
    iota_f32(tpos[:], b_set, [[128, NT]], 0, 1, NT)
    iota_f32(rpos[:], b_set, [[-128, NT]], T - 1, -1, NT)

    absd, u_, s_, s2, pl, mn = s8
    bl = [Buf(f"l{i}") for i in range(8)]
    kb.op("dve", lambda e: e.tensor_scalar(out=mn[:], in0=dec[:], scalar1=-1.0, scalar2=None, op0=OP.mult), reads=[cbuf], writes=[bl[5]])
    kb.op("dve", lambda e: e.tensor_tensor(out=absd[:], in0=dec[:], in1=mn[:], op=OP.max), reads=[cbuf, bl[5]], writes=[bl[0]])
    kb.op("act", lambda e: e.activation(out=u_[:], in_=absd[:], func=AF.Exp, scale=-1.0), reads=[bl[0]], writes=[bl[1]])
    kb.op("dve", lambda e: e.tensor_scalar(out=s_[:], in0=u_[:], scalar1=2.0, scalar2=None, op0=OP.add), reads=[bl[1]], writes=[bl[2]])
    kb.op("dve", lambda e: e.reciprocal(out=s_[:], in_=s_[:]), reads=[bl[2]], writes=[bl[2]])
    kb.op("dve", lambda e: e.tensor_tensor(out=s_[:], in0=s_[:], in1=u_[:], op=OP.mult), reads=[bl[1], bl[2]], writes=[bl[2]])
    kb.op("dve", lambda e: e.tensor_tensor(out=s2[:], in0=s_[:], in1=s_[:], op=OP.mult), reads=[bl[2]], writes=[bl[3]])
    kb.op("dve", lambda e: e.tensor_scalar(out=pl[:], in0=s2[:], scalar1=1.0 / 11, scalar2=1.0 / 9, op0=OP.mult, op1=OP.add), reads=[bl[3]], writes=[bl[4]])
    for cst in (1.0 / 7, 1.0 / 5, 1.0 / 3, 1.0):
        kb.op("dve", lambda e: e.tensor_tensor(out=pl[:], in0=pl[:], in1=s2[:], op=OP.mult), reads=[bl[3], bl[4]], writes=[bl[4]])
        kb.op("dve", lambda e, cst=cst: e.tensor_scalar(out=pl[:], in0=pl[:], scalar1=cst, scalar2=None, op0=OP.add), reads=[bl[4]], writes=[bl[4]])
    kb.op("dve", lambda e: e.tensor_tensor(out=pl[:], in0=pl[:], in1=s_[:], op=OP.mult), reads=[bl[2], bl[4]], writes=[bl[4]])
    kb.op("dve", lambda e: e.tensor_scalar(out=mn[:], in0=dec[:], scalar1=0.0, scalar2=None, op0=OP.min), reads=[cbuf], writes=[bl[5]])
    kb.op("dve", lambda e: e.scalar_tensor_tensor(out=lg[:], in0=pl[:], scalar=-2.0, in1=mn[:], op0=OP.mult, op1=OP.add), reads=[bl[4], bl[5]], writes=[b_lg])
    kb.op("act", lambda e: e.activation(out=g128[:], in_=lg[:], func=AF.Exp, scale=128.0), reads=[b_lg], partial=[b_tabs])
    LN_SCALE = math.log(DK ** -0.5)
    lnsc = ar.alloc("lnsc", [128, 1], F32)
    kb.op("dve", lambda e: e.memset(lnsc[:], LN_SCALE), partial=[b_set])
    for h in range(H):
        lgf = lg[:, h:h + 1]
        lgb = lg[:, 4 + h:5 + h]
        kb.op("act", lambda e, h=h, lgf=lgf: e.activation(out=xif[:, h, :], in_=fi1[:], func=AF.Exp, scale=lgf, bias=lnsc[:]), reads=[b_lg, b_set], partial=[b_tabs])
        kb.op("act", lambda e, h=h, lgb=lgb: e.activation(out=xib[:, h, :], in_=fr[:], func=AF.Exp, scale=lgb, bias=lnsc[:]), reads=[b_lg, b_set], partial=[b_tabs])
        kb.op("act", lambda e, h=h, lgf=lgf: e.activation(out=zcol[:, h:h + 1], in_=pr[:], func=AF.Exp, scale=lgf), reads=[b_lg, b_set], partial=[b_tabs])
        kb.op("act", lambda e, h=h, lgb=lgb: e.activation(out=zcol[:, 4 + h:5 + h], in_=pi_[:], func=AF.Exp, scale=lgb), reads=[b_lg, b_set], partial=[b_tabs])
        kb.op("act", lambda e, h=h, lgf=lgf: e.activation(out=zp[:, h, :], in_=rpos[:], func=AF.Exp, scale=lgf), reads=[b_lg, b_set], partial=[b_tabs])
        kb.op("act", lambda e, h=h, lgb=lgb: e.activation(out=zp[:, 4 + h, :], in_=tpos[:], func=AF.Exp, scale=lgb), reads=[b_lg, b_set], partial=[b_tabs])
        kb.op("act", lambda e, h=h, lgf=lgf: e.activation(out=wseg[:, h, :], in_=segw[:, 0:4], func=AF.Exp, scale=lgf), reads=[b_lg, cbuf], writes=[bt[1]])
        kb.op("dve", lambda e, h=h: e.tensor_tensor(out=wseg[:, h, :], in0=wseg[:, h, :], in1=segw[:, 4:8], op=OP.mult), reads=[bt[1], cbuf], partial=[b_tabs])
        kb.op("act", lambda e, h=h, lgb=lgb: e.activation(out=wseg[:, 4 + h, :], in_=segw[:, 8:12], func=AF.Exp, scale=lgb), reads=[b_lg, cbuf], writes=[bt[2]])
        kb.op("dve", lambda e, h=h: e.tensor_tensor(out=wseg[:, 4 + h, :], in0=wseg[:, 4 + h, :], in1=segw[:, 12:16], op=OP.mult), reads=[bt[2], cbuf], partial=[b_tabs])
        kb.op("dve", lambda e, lgb=lgb: e.tensor_scalar(out=tmpA[:], in0=jmi[:], scalar1=lgb, scalar2=None, op0=OP.mult), reads=[b_lg, b_set], writes=[bt[3]])
        kb.op("dve", lambda e, lgf=lgf: e.tensor_scalar(out=tmpB[:], in0=fi1[:], scalar1=lgf, scalar2=None, op0=OP.mult), reads=[b_lg, b_set], writes=[bt[4]])
        kb.op("dve", lambda e: e.tensor_tensor(out=tmpA[:], in0=tmpA[:], in1=tmpB[:], op=OP.subtract), reads=[bt[3], bt[4]], writes=[bt[3]])
        kb.op("dve", lambda e, lgf=lgf: e.tensor_scalar(out=tmpC[:], in0=pp1[:].to_broadcast([128, 128]), scalar1=lgf, scalar2=-1.0, op0=OP.mult, op1=OP.mult), reads=[b_lg, b_set], writes=[bt[5]])
        kb.op("dve", lambda e: e.tensor_tensor(out=tmpC[:], in0=tmpC[:], in1=tmpA[:], op=OP.subtract), reads=[bt[5], bt[3]], writes=[bt[5]])
        kb.op("dve", lambda e: e.tensor_single_scalar(out=tmpM[:], in_=jmi[:], scalar=0.0, op=OP.is_le), reads=[b_set], writes=[bt[6]])
        kb.op("dve", lambda e: e.tensor_tensor(out=tmpC[:], in0=tmpC[:], in1=tmpM[:], op=OP.mult), reads=[bt[5], bt[6]], writes=[bt[5]])
        kb.op("dve", lambda e: e.tensor_tensor(out=tmpA[:], in0=tmpA[:], in1=tmpC[:], op=OP.add), reads=[bt[3], bt[5]], writes=[bt[3]])
        kb.op("act", lambda e, h=h: e.activation(out=maskT[:, h, :], in_=tmpA[:], func=AF.Exp), reads=[bt[3]], partial=[b_tabs])
    kb.barrier()
    ar.release(m_setup)

    if debug.get("dump_consts"):
        dump("lg", lg[:], b_lg, [128, 8], F32)
        dump("maskT", maskT[:], b_tabs, [128, H, 128], F32)
        dump("xif", xif[:], b_tabs, [128, H, 128], F32)
        dump("xib", xib[:], b_tabs, [128, H, 128], F32)
        dump("zp", zp[:], b_tabs, [128, 8, NT], F32)
        dump("zcol", zcol[:], b_tabs, [128, 8], F32)
        dump("wseg", wseg[:], b_tabs, [128, 8, 4], F32)
        dump("g128", g128[:], b_tabs, [128, 8], F32)
    if stop <= 0:
        return finish()

    hT = ar.alloc("hT", [128, 8, T], BF16)
    hTh = ar.alloc("hTh", [128, 8, 2], BF16)
    b_hT = Buf("hT")
    b_hTh = Buf("hTh")

    def norm_to_T(xt, b_xt, np_, dstT_ap_fn, b_dst, gT, junk, b_junk, xs, b_xs, st, b_st, pbank, defer=False):
        kb.op("act", lambda e: e.activation(out=junk[0:np_, :], in_=xt[0:np_, :], func=AF.Square, accum_out=st[0:np_, 0:1]), reads=[b_xt], writes=[b_junk, b_st], multi=True)
        kb.op("act", lambda e: e.activation(out=st[0:np_, 1:2], in_=st[0:np_, 0:1], func=AF.Sqrt, scale=1.0 / D, bias=epsc[0:np_, :]), reads=[b_st], writes=[b_st])
        kb.op("dve", lambda e: e.reciprocal(out=st[0:np_, 2:3], in_=st[0:np_, 1:2]), reads=[b_st], writes=[b_st])
        kb.op("dve", lambda e: e.tensor_scalar(out=xs[0:np_, :], in0=xt[0:np_, :], scalar1=st[0:np_, 2:3], scalar2=None, op0=OP.mult), reads=[b_xt, b_st], writes=[b_xs])
        if not defer:
            norm_B(np_, dstT_ap_fn, b_dst, gT, xs, b_xs, pbank)

    def norm_B(np_, dstT_ap_fn, b_dst, gT, xs, b_xs, pbank):
        pv = psb(pbank)[:, 0:8 * np_].rearrange("p (c t) -> p c t", c=8)
        for c in range(8):
            kb.op("pe", lambda e, c=c: e.transpose(out=pv[:, c, :], in_=xs[0:np_, c * 128:(c + 1) * 128], identity=ident[0:np_, 0:np_]),
                  reads=[b_xs, b_ident], writes=[PB[pbank]] if c == 0 else (), partial=[PB[pbank]] if c else (), inc=(c == 7))
        kb.op("dve", lambda e: e.tensor_tensor(out=dstT_ap_fn(), in0=pv, in1=gT[:, :].unsqueeze(2).to_broadcast([128, 8, np_]), op=OP.mult),
              reads=[PB[pbank], cbuf], partial=[b_dst])

    epsc = ar.alloc("epsc", [128, 1], F32)
    b_eps = Buf("eps")
    kb.op("dve", lambda e: e.memset(epsc[:], EPS), writes=[b_eps])
    kb.barrier(engines=("act", "dve"))

    m1 = ar.mark()
    NXB = 3
    xts = [ar.alloc(f"xt{i}", [128, D], F32) for i in range(NXB)]
    b_xts = [Buf(f"xt{i}") for i in range(NXB)]
    junk = ar.alloc("junk", [128, D], BF16)
    b_junk = Buf("junk")
    xss = [ar.alloc(f"xs{i}", [128, D], BF16) for i in range(2)]
    b_xss = [Buf(f"xs{i}") for i in range(2)]
    sts = [ar.alloc(f"st{i}", [128, 4], F32) for i in range(2)]
    b_sts = [Buf(f"st{i}") for i in range(2)]
    kb.dma("sp", xts[0][0:2, :], x_halo, b_xts[0], reads=[b_in], writes=[b_xts[0]])
    norm_to_T(xts[0], b_xts[0], 2, lambda: hTh[:, :, :], b_hTh, gmixT, junk, b_junk, xss[0], b_xss[0], sts[0], b_sts[0], 0)
    for i in range(min(NXB - 1, NT)):
        kb.dma("sp", xts[(i + 1) % NXB][:], x_seg[i * 128:(i + 1) * 128, :], b_xts[(i + 1) % NXB], reads=[b_in], writes=[b_xts[(i + 1) % NXB]])
    pend1 = []
    for i in range(NT):
        j = i + NXB - 1
        if j < NT:
            kb.dma("sp", xts[(j + 1) % NXB][:], x_seg[j * 128:(j + 1) * 128, :], b_xts[(j + 1) % NXB], reads=[b_in], writes=[b_xts[(j + 1) % NXB]])
        s = (i + 1) % NXB
        norm_to_T(xts[s], b_xts[s], 128, None, b_hT, gmixT, junk, b_junk,
                  xss[i % 2], b_xss[i % 2], sts[i % 2], b_sts[i % 2], i % 2, defer=True)
        pend1.append((lambda i=i: hT[:, :, i * 128:(i + 1) * 128], i % 2))
        if len(pend1) > 1:
            fn_, q_ = pend1.pop(0)
            norm_B(128, fn_, b_hT, gmixT, xss[q_], b_xss[q_], q_)
    for fn_, q_ in pend1:
        norm_B(128, fn_, b_hT, gmixT, xss[q_], b_xss[q_], q_)
    kb.barrier()
    ar.release(m1)
    if debug.get("dump_hT"):
        dump("hT", hT[:], b_hT, [128, 8, T], BF16)
        dump("hTh", hTh[:], b_hTh, [128, 8, 2], BF16)
    if stop <= 1:
        return finish()

    m_ret = ar.mark()
    cosT = ar.alloc("cosT", [128, T], F32)
    sinT = ar.alloc("sinT", [128, T], F32)
    b_rot = Buf("rot")
    m_r0 = ar.mark()
    ang = ar.alloc("ang", [128, T], F32)
    kk = ar.alloc("kk", [128, T], F32)
    ki = ar.alloc("ki", [128, T], I32)
    b_ang, b_kk, b_ki = Buf("ang"), Buf("kk"), Buf("ki")
    kb.dma("sp", ang[:], pos_in[0:1, :].partition_broadcast(128), b_ang, reads=[b_in], writes=[b_ang])
    kb.op("dve", lambda e: e.tensor_scalar(out=ang[:], in0=ang[:], scalar1=freq[:, 0:1], scalar2=None, op0=OP.mult), reads=[b_ang, cbuf], writes=[b_ang])
    hpi = ar.alloc("hpi", [128, 1], F32)
    b_hpi = Buf("hpi")
    kb.op("dve", lambda e: e.memset(hpi[:], math.pi / 2), writes=[b_hpi])
    kb.op("dve", lambda e: e.tensor_scalar(out=kk[:], in0=ang[:], scalar1=1.0 / (2 * math.pi), scalar2=None, op0=OP.mult), reads=[b_ang], writes=[b_kk])
    kb.op("dve", lambda e: e.tensor_copy(out=ki[:], in_=kk[:]), reads=[b_kk], writes=[b_ki])
    kb.op("dve", lambda e: e.tensor_copy(out=kk[:], in_=ki[:]), reads=[b_ki], writes=[b_kk])
    kb.op("dve", lambda e: e.scalar_tensor_tensor(out=sinT[:], in0=kk[:], scalar=-TWO_PI_HI, in1=ang[:], op0=OP.mult, op1=OP.add), reads=[b_kk, b_ang], partial=[b_rot])
    kb.op("dve", lambda e: e.scalar_tensor_tensor(out=sinT[:], in0=kk[:], scalar=-TWO_PI_LO, in1=sinT[:], op0=OP.mult, op1=OP.add), reads=[b_kk, b_rot], partial=[b_rot])
    kb.op("dve", lambda e: e.tensor_scalar(out=sinT[:], in0=sinT[:], scalar1=3.141592, scalar2=-3.141592, op0=OP.min, op1=OP.max), reads=[b_rot], partial=[b_rot])
    kb.op("act", lambda e: e.activation(out=cosT[:], in_=sinT[:], func=AF.Abs), reads=[b_rot], partial=[b_rot])
    kb.op("act", lambda e: e.activation(out=cosT[:], in_=cosT[:], func=AF.Sin, scale=-1.0, bias=hpi[:]), reads=[b_rot, b_hpi], partial=[b_rot])
    kb.op("act", lambda e: e.activation(out=sinT[:], in_=sinT[:], func=AF.Sin), reads=[b_rot], partial=[b_rot])
    kb.barrier()
    ar.release(m_r0)
    if debug.get("dump_rot"):
        dump("cosT", cosT[:], b_rot, [128, T], F32)
        dump("sinT", sinT[:], b_rot, [128, T], F32)

    if stop <= 1.5:
        return finish()

    def rotary(ps0, ps1, g, outs, b_outs, tmps, b_tmps, scale_tabs=None):
        gs = slice(g * 512, (g + 1) * 512)
        ta, tb, tc_, td = tmps
        bta, btb, btc, btd = b_tmps
        kb.op("dve", lambda e: e.tensor_tensor(out=ta[:], in0=psf(ps0), in1=cosT[:, gs], op=OP.mult), reads=[PB[ps0], b_rot], writes=[bta])
        kb.op("dve", lambda e: e.tensor_tensor(out=tb[:], in0=psf(ps1), in1=sinT[:, gs], op=OP.mult), reads=[PB[ps1], b_rot], writes=[btb])
        kb.op("dve", lambda e: e.tensor_tensor(out=tc_[:], in0=psf(ps0), in1=sinT[:, gs], op=OP.mult), reads=[PB[ps0], b_rot], writes=[btc])
        kb.op("dve", lambda e: e.tensor_tensor(out=td[:], in0=psf(ps1), in1=cosT[:, gs], op=OP.mult), reads=[PB[ps1], b_rot], writes=[btd])
        if scale_tabs is None:
            kb.op("pool", lambda e: e.tensor_tensor(out=outs[0][:, 0, gs], in0=ta[:], in1=tb[:], op=OP.subtract), reads=[bta, btb], partial=[b_outs[0]])
            kb.op("pool", lambda e: e.tensor_tensor(out=outs[0][:, 1, gs], in0=tc_[:], in1=td[:], op=OP.add), reads=[btc, btd], partial=[b_outs[0]])
        else:
            kb.op("dve", lambda e: e.tensor_tensor(out=ta[:], in0=ta[:], in1=tb[:], op=OP.subtract), reads=[bta, btb], writes=[bta])
            kb.op("dve", lambda e: e.tensor_tensor(out=tc_[:], in0=tc_[:], in1=td[:], op=OP.add), reads=[btc, btd], writes=[btc])
            for k, tab in enumerate(scale_tabs):
                tb4 = tab.unsqueeze(1).to_broadcast([128, 4, 128])
                eng_ = "pool" if k == 0 else "dve"
                kb.op(eng_, lambda e, k=k, tb4=tb4: e.tensor_tensor(out=outs[k][:, 0, gs].rearrange("p (c t) -> p c t", c=4), in0=ta[:].rearrange("p (c t) -> p c t", c=4), in1=tb4, op=OP.mult),
                      reads=[bta, b_tabs], partial=[b_outs[k]])
                kb.op(eng_, lambda e, k=k, tb4=tb4: e.tensor_tensor(out=outs[k][:, 1, gs].rearrange("p (c t) -> p c t", c=4), in0=tc_[:].rearrange("p (c t) -> p c t", c=4), in1=tb4, op=OP.mult),
                      reads=[btc, b_tabs], partial=[b_outs[k]])

    def proj_fm(wt, col0, ncol128, g, banks, b_w):
        gs = slice(g * 512, (g + 1) * 512)
        for j in range(ncol128):
            for dc in range(8):
                kb.op("pe", lambda e, j=j, dc=dc: e.matmul(psf(banks[j]), lhsT=wt[:, dc, col0 + j * 128: col0 + (j + 1) * 128], rhs=hT[:, dc, gs], start=(dc == 0), stop=(dc == 7)),
                      reads=[b_w, b_hT], writes=[PB[banks[j]]] if dc == 0 else (), partial=[PB[banks[j]]] if dc else (), inc=(dc == 7))

    def emit_ag(h):
        kb._emit_waits("pool", [b_cc_src[h]], [b_cc_dst[h]], [])
        nc.gpsimd.collective_compute("AllGather", OP.bypass, replica_groups=[[0, 1, 2, 3], [4, 5, 6, 7]],
                                     ins=[cc_src[h].opt()], outs=[cc_dst[h].opt()]).then_inc(cc_sem, 1)
        kb.cnt["cc"] += 1
        kb.all["cc"] = kb.cnt["cc"]
        b_cc_dst[h].w = {"cc": kb.cnt["cc"]}
        b_cc_src[h].r["cc"] = kb.cnt["cc"]

    m2 = ar.mark()
    wk = [ar.alloc(f"wk{i}", [128, 8, DK], BF16) for i in range(2)]
    wv = [ar.alloc(f"wv{i}", [128, 8, DV], BF16) for i in range(2)]
    b_wk = [Buf(f"wk{i}") for i in range(2)]
    b_wv = [Buf(f"wv{i}") for i in range(2)]
    krT = [ar.alloc(f"krT{i}", [128, 2, T], BF16) for i in range(2)]
    b_krT = [Buf(f"krT{i}") for i in range(2)]
    Vh = [ar.alloc(f"Vh{i}", [128, NT, DV], BF16) for i in range(2)]
    b_Vh = [Buf(f"Vh{i}") for i in range(2)]
    rtm = [[ar.alloc(f"rt{i}{k}", [128, 512], F32) for k in range(4)] for i in range(2)]
    b_rtm = [[Buf(f"rt{i}{k}") for k in range(4)] for i in range(2)]
    kz = [[ar.alloc(f"kz{i}{k}", [128, DK], BF16) for k in range(2)] for i in range(2)]
    b_kz = [[Buf(f"kz{i}{k}") for k in range(2)] for i in range(2)]
    Lsb = ar.alloc("Lsb", [128, 4, DV], F32)
    b_Lsb = Buf("Lsb")

    def load_kv_w(h):
        s = h % 2
        kb.dma("pool", wk[s][:], w_in_v[:, :, C_K + h * DK: C_K + (h + 1) * DK], b_wk[s], reads=[b_in], writes=[b_wk[s]])
        kb.dma("pool", wv[s][:], w_in_v[:, :, C_V + h * DV: C_V + (h + 1) * DV], b_wv[s], reads=[b_in], writes=[b_wv[s]])

    load_kv_w(0)
    for h in range(H):
        s = h % 2
        if h + 1 < H:
            load_kv_w(h + 1)
        for g in range(NG):
            bk = (0, 1) if g % 2 == 0 else (2, 3)
            proj_fm(wk[s], 0, 2, g, bk, b_wk[s])
            rotary(bk[0], bk[1], g, [krT[s]], [b_krT[s]], rtm[g % 2], b_rtm[g % 2])
        if debug.get("p2cut") == 1:
            return finish()
        kb.dma("sp", kT_d[h].rearrange("p (j t) -> p j t", j=2), krT[s][:], b_krT[s], reads=[b_krT[s]], writes=[b_kT_d[h]])
        if debug.get("p2cut") == 2:
            return finish()

        def v_and_t(i, s=s, h=h):
            vb = 4 + (i % 2)
            for dc in range(8):
                kb.op("pe", lambda e, dc=dc: e.matmul(psf(vb), lhsT=hT[:, dc, i * 128:(i + 1) * 128], rhs=wv[s][:, dc, :], start=(dc == 0), stop=(dc == 7)),
                      reads=[b_wv[s], b_hT], writes=[PB[vb]] if dc == 0 else (), partial=[PB[vb]] if dc else (), inc=(dc == 7))
            kb.op("act", lambda e: e.activation(out=Vh[s][:, i, :], in_=psf(vb), func=AF.Copy), reads=[PB[vb]], partial=[b_Vh[s]])
            tb_ = 6 + (i % 2)
            tv = psb(tb_)[:, 0:DK]
            for j in range(2):
                kb.op("pe", lambda e, j=j: e.transpose(out=tv[:, j * 128:(j + 1) * 128], in_=krT[s][:, j, i * 128:(i + 1) * 128], identity=ident[:]),
                      reads=[b_krT[s], b_ident], writes=[PB[tb_]] if j == 0 else (), partial=[PB[tb_]] if j else (), inc=(j == 1))
            kb.op("dve", lambda e: e.tensor_scalar(out=kz[i % 2][0][:], in0=tv, scalar1=zp[:, h, i:i + 1], scalar2=None, op0=OP.mult), reads=[PB[tb_], b_tabs], writes=[b_kz[i % 2][0]])
            kb.op("dve", lambda e: e.tensor_scalar(out=kz[i % 2][1][:], in0=tv, scalar1=zp[:, 4 + h, i:i + 1], scalar2=None, op0=OP.mult), reads=[PB[tb_], b_tabs], writes=[b_kz[i % 2][1]])

        def l_acc(i, s=s):
            for d_ in range(2):
                for c2 in range(2):
                    bank = d_ * 2 + c2
                    kb.op("pe", lambda e, d_=d_, c2=c2, bank=bank: e.matmul(psf(bank), lhsT=kz[i % 2][d_][:, c2 * 128:(c2 + 1) * 128], rhs=Vh[s][:, i, :], start=(i == 0), stop=(i == NT - 1)),
                          reads=[b_kz[i % 2][d_], b_Vh[s]], writes=[PB[bank]] if i == 0 else (), partial=[PB[bank]] if i else (), inc=(d_ == 1 and c2 == 1))

        v_and_t(0)
        if debug.get("p2cut") == 3:
            return finish()
        for i in range(NT):
            if i + 1 < NT:
                v_and_t(i + 1)
            l_acc(i)
        if debug.get("p2cut") == 4:
            return finish()
        for bank in range(4):
            kb.op("act" if bank % 2 else "dve",
                  (lambda e, bank=bank: e.activation(out=Lsb[:, bank, :], in_=psf(bank), func=AF.Copy)) if bank % 2 else
                  (lambda e, bank=bank: e.tensor_copy(out=Lsb[:, bank, :], in_=psf(bank))),
                  reads=[PB[bank]], partial=[b_Lsb])
        for d_ in range(2):
            kb.dma("sp", cc_src[h].rearrange("(d c p) n -> d p c n", d=2, c=2, p=128)[d_], Lsb[:, d_ * 2:(d_ + 1) * 2, :], b_Lsb,
                   reads=[b_Lsb], partial=[b_cc_src[h]])
        if not debug.get("no_cc") and (h < H - 1 or stop <= 2):
            emit_ag(h)
        kb.dma("sp", V_d[h].rearrange("p (i n) -> p i n", i=NT), Vh[s][:], b_Vh[s], reads=[b_Vh[s]], writes=[b_V_d[h]])
    kb.barrier()
    ar.release(m2)
    if debug.get("dump_p1"):
        t = nc.dram_tensor("dbg_ccsrc", [2 * H * 2 * 128, DV], F32, kind="ExternalOutput").ap()
        stg = ar.alloc("stg", [128, 16, DV], F32)
        b_stg = Buf("stg")
        for h in range(H):
            for d_ in range(2):
                kb.dma("sp", stg[:, (d_ * H + h) * 2:(d_ * H + h) * 2 + 2, :], cc_src[h].rearrange("(d c p) n -> d p c n", d=2, c=2, p=128)[d_], b_stg, reads=[b_cc_src[h]], partial=[b_stg])
        kb.dma("sp", t.rearrange("(r p) n -> p r n", p=128), stg[:], b_stg, reads=[b_stg])
        dumps.append("dbg_ccsrc")
        kb.barrier()
    if stop <= 2:
        return finish()

    wq = [ar.alloc(f"wq{i}", [128, 8, DK], BF16) for i in range(2)]
    wg = [ar.alloc(f"wg{i}", [128, 8, DV], BF16) for i in range(2)]
    b_wq = [Buf(f"wq{i}") for i in range(2)]
    b_wg = [Buf(f"wgs{i}") for i in range(2)]
    m4 = ar.mark()
    zT = ar.alloc("zT", [128, 8, T], BF16)
    b_zT = Buf("zT")
    NWC = 3
    wcv = [ar.alloc(f"wcv{i}", [128, 8, 3, 128], BF16) for i in range(NWC)]
    b_wcv = [Buf(f"wcv{i}") for i in range(NWC)]
    ubuf = [ar.alloc(f"u{i}", [128, T + 2], F32) for i in range(2)]
    b_u = [Buf(f"u{i}") for i in range(2)]
    gbb = [ar.alloc(f"gb{i}", [128, T], F32) for i in range(2)]
    b_gb = [Buf(f"gb{i}") for i in range(2)]
    acc = [ar.alloc(f"acc{i}", [128, T], F32) for i in range(2)]
    b_acc = [Buf(f"acc{i}") for i in range(2)]
    gct = [ar.alloc(f"gct{i}", [128, 512], F32) for i in range(2)]
    b_gct = [Buf(f"gct{i}") for i in range(2)]
    hx = [ar.alloc(f"hx{i}", [128, 2], F32) for i in range(2)]
    b_hx = [Buf(f"hx{i}") for i in range(2)]

    def load_conv_w(c):
        s = c % NWC
        for k, col in enumerate((C_XC, C_GC, C_GB)):
            kb.dma("pool", wcv[s][:, :, k, :], w_in_v[:, :, col + c * 128: col + (c + 1) * 128], b_wcv[s], reads=[b_in],
                   writes=[b_wcv[s]] if k == 0 else (), partial=[b_wcv[s]] if k else ())

    cc_v = [cc_dst[h].rearrange("(r d c p) n -> r d p c n", r=NSEG, d=2, c=2, p=128) for h in range(H)]
    sacc = [ar.alloc(f"sacc{i}", [128, 2, DV], F32) for i in range(2)]
    slin = [ar.alloc(f"slin{i}", [128, 2, DV], F32) for i in range(2)]
    b_sacc = [Buf(f"sacc{i}") for i in range(2)]
    b_slin = [Buf(f"slin{i}") for i in range(2)]
    sin_k = [0]

    def sin_pre(h, d_):
        a_ = (h * 2 + d_) % 2
        accv = sacc[a_][:]
        ranks = (0, 1, 2) if d_ == 0 else (1, 2, 3)
        for ri, r in enumerate(ranks):
            lb = sin_k[0] % 2
            sin_k[0] += 1
            linv = slin[lb][:]
            kb.dma("sp", linv, cc_v[h][r, d_], b_slin[lb], reads=[b_cc_dst[h]], writes=[b_slin[lb]])
            wsc = wseg[:, d_ * 4 + h, r:r + 1]
            if ri == 0:
                kb.op("dve", lambda e, wsc=wsc, accv=accv, linv=linv: e.tensor_scalar(out=accv, in0=linv, scalar1=wsc, scalar2=None, op0=OP.mult),
                      reads=[b_slin[lb], b_tabs], writes=[b_sacc[a_]])
            else:
                kb.op("dve", lambda e, wsc=wsc, accv=accv, linv=linv: e.scalar_tensor_tensor(out=accv, in0=linv, scalar=wsc, in1=accv, op0=OP.mult, op1=OP.add),
                      reads=[b_slin[lb], b_tabs, b_sacc[a_]], writes=[b_sacc[a_]])
        kb.dma("sp", sin_d[h * 2 + d_].rearrange("p (c n) -> p c n", c=2), accv, b_sacc[a_], reads=[b_sacc[a_]], writes=[b_sin_d[h * 2 + d_]])

    load_conv_w(0)
    load_conv_w(1)
    for c in range(8):
        s = c % 2
        ws = c % NWC
        if c + 2 < 8:
            load_conv_w(c + 2)
        if c == 1 and not debug.get("no_cc"):
            emit_ag(H - 1)
        for k in range(2):
            for dc in range(8):
                kb.op("pe", lambda e, k=k, dc=dc: e.matmul(ps_all[:, 6, 2 * k:2 * k + 2], lhsT=wcv[ws][:, dc, k, :], rhs=hTh[:, dc, :], start=(dc == 0), stop=(dc == 7)),
                      reads=[b_wcv[ws], b_hTh], writes=[PB[6]] if (dc == 0 and k == 0) else (), partial=[PB[6]] if (dc or k) else (), inc=(dc == 7 and k == 1))
        kb.op("act", lambda e: e.activation(out=hx[s][:], in_=ps_all[:, 6, 2:4], func=AF.Copy), reads=[PB[6]], writes=[b_hx[s]])
        kb.op("dve", lambda e: e.tensor_tensor(out=ubuf[s][:, 0:T + 2:T + 1], in0=ps_all[:, 6, 0:2], in1=hx[s][:], op=OP.mult), reads=[PB[6], b_hx[s]], writes=[b_u[s]])
        for g in range(NG):
            gs = slice(g * 512, (g + 1) * 512)
            bk = (0, 1, 2) if g % 2 == 0 else (3, 4, 5)
            for k in range(3):
                for dc in range(8):
                    kb.op("pe", lambda e, k=k, dc=dc: e.matmul(psf(bk[k]), lhsT=wcv[ws][:, dc, k, :], rhs=hT[:, dc, gs], start=(dc == 0), stop=(dc == 7)),
                          reads=[b_wcv[ws], b_hT], writes=[PB[bk[k]]] if dc == 0 else (), partial=[PB[bk[k]]] if dc else (), inc=(dc == 7))
            kb.op("act", lambda e: e.activation(out=gct[g % 2][:], in_=psf(bk[1]), func=AF.Copy), reads=[PB[bk[1]]], writes=[b_gct[g % 2]])
            kb.op("act", lambda e: e.activation(out=gbb[s][:, gs], in_=psf(bk[2]), func=AF.Copy), reads=[PB[bk[2]]], partial=[b_gb[s]])
            kb.op("dve", lambda e: e.tensor_tensor(out=ubuf[s][:, 1 + g * 512: 1 + (g + 1) * 512], in0=psf(bk[0]), in1=gct[g % 2][:], op=OP.mult),
                  reads=[PB[bk[0]], b_gct[g % 2]], partial=[b_u[s]])
        w0 = wconvT[:, c * 3 + 0: c * 3 + 1]
        w1 = wconvT[:, c * 3 + 1: c * 3 + 2]
        w2 = wconvT[:, c * 3 + 2: c * 3 + 3]
        kb.op("dve", lambda e: e.tensor_scalar(out=acc[s][:], in0=ubuf[s][:, 0:T], scalar1=w0, scalar2=None, op0=OP.mult), reads=[b_u[s], cbuf], writes=[b_acc[s]])
        kb.op("dve", lambda e: e.scalar_tensor_tensor(out=acc[s][:], in0=ubuf[s][:, 1:T + 1], scalar=w1, in1=acc[s][:], op0=OP.mult, op1=OP.add), reads=[b_u[s], cbuf, b_acc[s]], writes=[b_acc[s]])
        kb.op("dve", lambda e: e.scalar_tensor_tensor(out=acc[s][:], in0=ubuf[s][:, 2:T + 2], scalar=w2, in1=acc[s][:], op0=OP.mult, op1=OP.add), reads=[b_u[s], cbuf, b_acc[s]], writes=[b_acc[s]])
        kb.op("dve", lambda e: e.tensor_tensor(out=zT[:, c, :], in0=acc[s][:], in1=gbb[s][:], op=OP.mult), reads=[b_acc[s], b_gb[s]], partial=[b_zT])
        sin_pre(c // 2, c % 2)
    kb.dma("sp", z_d.rearrange("p (c t) -> p c t", c=8), zT[:], b_zT, reads=[b_zT], writes=[b_z_d])
    def load_qg_w(h):
        s = h % 2
        kb.dma("pool", wq[s][:], w_in_v[:, :, C_Q + h * DK: C_Q + (h + 1) * DK], b_wq[s], reads=[b_in], writes=[b_wq[s]])
        kb.dma("pool", wg[s][:], w_in_v[:, :, C_GSW + h * DV: C_GSW + (h + 1) * DV], b_wg[s], reads=[b_in], writes=[b_wg[s]])

    if stop > 4:
        load_qg_w(0)
    kb.barrier()
    ar.release(m4)
    if debug.get("dump_z"):
        stg = ar.alloc("stgz", [128, 8, T], BF16)
        b_stg = Buf("stgz")
        kb.dma("sp", stg[:], z_d.rearrange("p (c t) -> p c t", c=8), b_stg, reads=[b_z_d], writes=[b_stg])
        dump("zT", stg[:], b_stg, [128, 8, T], BF16)
        kb.barrier()
    if stop <= 4:
        return finish()

    m5 = ar.mark()
    gretb = [ar.alloc(f"gretb{i}", [128, DV], F32) for i in range(2)]
    b_gretb = [Buf(f"gretb{i}") for i in range(2)]
    krT2 = ar.alloc("krT2", [128, 2, T], BF16)
    V2 = ar.alloc("V2", [128, NT, DV], BF16)
    b_krT2, b_V2 = Buf("krT2"), Buf("V2")
    QfT = ar.alloc("QfT", [128, 2, T], BF16)
    QbT = ar.alloc("QbT", [128, 2, T], BF16)
    b_QfT, b_QbT = Buf("QfT"), Buf("QbT")
    SbAll = ar.alloc("SbAll", [128, NT, 2, DV], BF16)
    b_SbAll = [Buf(f"SbAll{c}") for c in range(NT)]
    Sst = ar.alloc("Sst", [128, 2, DV], F32)
    b_Sst = Buf("Sst")
    Sfb = [ar.alloc(f"Sfb{i}", [128, 2, DV], BF16) for i in range(2)]
    b_Sfb = [Buf(f"Sfb{i}") for i in range(2)]
    gTh = [ar.alloc(f"gTh{i}", [128, 4, 512], BF16) for i in range(2)]
    b_gTh = [Buf(f"gTh{i}") for i in range(2)]
    sgAll = ar.alloc("sgAll", [128, NT, DV], BF16)
    b_sgAll = [Buf(f"sgAll{c}") for c in range(NT)]
    rtL = ar.alloc("rtL", [128, 4, 512], F32)
    rt2 = [rtL[:, k, :] for k in range(4)]
    b_rt2 = [Buf(f"r2{k}") for k in range(4)]
    Lin = [rtL[:, 0:2, :], None]
    b_mv2 = [Buf(f"mv2{i}") for i in range(2)]
    b_mv3 = [Buf(f"mv3{i}") for i in range(2)]
    kz2 = [ar.alloc(f"kz2{i}", [128, DK], BF16) for i in range(2)]
    b_kz2 = [Buf(f"kz2{i}") for i in range(2)]
    PTs = [ar.alloc(f"PT{i}", [128, 128], BF16) for i in range(2)]
    b_PTs = [Buf(f"PT{i}") for i in range(2)]
    st6 = [ar.alloc(f"st6{i}", [128, 6], F32) for i in range(2)]
    mv = [ar.alloc(f"mv{i}", [128, 4], F32) for i in range(2)]
    b_st6 = [Buf(f"st6{i}") for i in range(2)]
    b_mv = [Buf(f"mv{i}") for i in range(2)]
    ynL = ar.alloc("ynL", [128, 2, DV], F32)
    yn = [ynL[:, i, :] for i in range(2)]
    gat = [ar.alloc(f"gat{i}", [128, DV], BF16) for i in range(2)]
    b_yn = [Buf(f"yn{i}") for i in range(2)]
    b_sg = [Buf(f"sg{i}") for i in range(2)]
    b_gat = [Buf(f"gat{i}") for i in range(2)]

    def incoming_state(h, d_):
        kb.dma("sp", Sst[:], sin_d[h * 2 + d_].rearrange("p (c n) -> p c n", c=2), b_Sst, reads=[b_sin_d[h * 2 + d_]], writes=[b_Sst])

    def kt_scaled(c, zc, slot):
        tv = psb(3)[:, 0:DK]
        for j in range(2):
            kb.op("pe", lambda e, j=j: e.transpose(out=tv[:, j * 128:(j + 1) * 128], in_=krT2[:, j, c * 128:(c + 1) * 128], identity=ident[:]),
                  reads=[b_krT2, b_ident], writes=[PBx["kt"]] if j == 0 else (), partial=[PBx["kt"]] if j else (), inc=(j == 1))
        kb.op("dve", lambda e: e.tensor_scalar(out=kz2[slot][:], in0=tv, scalar1=zc, scalar2=None, op0=OP.mult), reads=[PBx["kt"], b_tabs], writes=[b_kz2[slot]])

    PBx = {"kt": PB[3], "pt": PB[2], "gt": PB[4]}

    def state_update(c, slot, ubank, gcol):
        for c2 in range(2):
            kb.op("pe", lambda e, c2=c2: e.matmul(psf(ubank + c2), lhsT=kz2[slot][:, c2 * 128:(c2 + 1) * 128], rhs=V2[:, c, :], start=True, stop=True),
                  reads=[b_kz2[slot], b_V2], writes=[PB[ubank + c2]], inc=(c2 == 1))
        kb.op("dve", lambda e: e.scalar_tensor_tensor(out=Sst[:], in0=Sst[:], scalar=gcol, in1=ps_all[:, ubank:ubank + 2, :], op0=OP.mult, op1=OP.add),
              reads=[b_Sst, PB[ubank], PB[ubank + 1], b_tabs], writes=[b_Sst])

    for h in range(H):
        s = h % 2
        if h + 1 < H:
            load_qg_w(h + 1)
        kb.dma("sp", gretb[s][:], gret_in[0:1, h * DV:(h + 1) * DV].partition_broadcast(128), b_gretb[s], reads=[b_in], writes=[b_gretb[s]])
        kb.dma("sp", krT2[:], kT_d[h].rearrange("p (j t) -> p j t", j=2), b_krT2, reads=[b_kT_d[h]], writes=[b_krT2])
        kb.dma("sp", V2[:], V_d[h].rearrange("p (i n) -> p i n", i=NT), b_V2, reads=[b_V_d[h]], writes=[b_V2])
        for g in range(NG):
            bk = (0, 1) if g % 2 == 0 else (2, 3)
            proj_fm(wq[s], 0, 2, g, bk, b_wq[s])
            rotary(bk[0], bk[1], g, [QfT, QbT], [b_QfT, b_QbT], rt2, b_rt2, scale_tabs=[xif[:, h, :], xib[:, h, :]])
        incoming_state(h, 1)
        for c in range(NT - 1, -1, -1):
            cs = slice(c * 128, (c + 1) * 128)
            kb.op("act", lambda e, c=c: e.activation(out=SbAll[:, c, :, :], in_=Sst[:], func=AF.Copy), reads=[b_Sst], writes=[b_SbAll[c]])
            if c > 0:
                kt_scaled(c, zcol[:, 4 + h:5 + h], c % 2)
                state_update(c, c % 2, 0, g128[:, 4 + h:5 + h])
            gb_ = 6 + (c % 2)
            for dc in range(8):
                kb.op("pe", lambda e, dc=dc: e.matmul(psf(gb_), lhsT=hT[:, dc, cs], rhs=wg[s][:, dc, :], start=(dc == 0), stop=(dc == 7)),
                      reads=[b_wg[s], b_hT], writes=[PB[gb_]] if dc == 0 else (), partial=[PB[gb_]] if dc else (), inc=(dc == 7))
            kb.op("act", lambda e, c=c: e.activation(out=yn[c % 2][:], in_=psf(gb_), func=AF.Silu), reads=[PB[gb_]], writes=[b_yn[c % 2]])
            kb.op("pool", lambda e, c=c: e.tensor_tensor(out=sgAll[:, c, :], in0=yn[c % 2][:], in1=gretb[s][:], op=OP.mult), reads=[b_yn[c % 2], b_gretb[s]], writes=[b_sgAll[c]])
        incoming_state(h, 0)
        kb.op("act", lambda e: e.activation(out=Sfb[0][:], in_=Sst[:], func=AF.Copy), reads=[b_Sst], writes=[b_Sfb[0]])

        def gated_T(c, h=h):
            p2 = c % 2
            gq, ci = divmod(c, 4)
            slot = gq % 2
            for half in range(2):
                tv = psb(4)[:, half * 256:(half + 1) * 256].rearrange("p (a t) -> p a t", a=2)
                for a in range(2):
                    ec = half * 2 + a
                    kb.op("pe", lambda e, a=a, ec=ec, tv=tv: e.transpose(out=tv[:, a, :], in_=gat[p2][:, ec * 128:(ec + 1) * 128], identity=ident[:]),
                          reads=[b_gat[p2], b_ident], writes=[PBx["gt"]] if (half == 0 and a == 0) else (), partial=[PBx["gt"]] if (half or a) else (), inc=(a == 1 and half == 1))
            tv4 = psb(4)[:, 0:512].rearrange("p (a t) -> p a t", a=4)
            kb.op("act", lambda e: e.activation(out=gTh[slot][:, :, ci * 128:(ci + 1) * 128], in_=tv4, func=AF.Copy), reads=[PBx["gt"]], partial=[b_gTh[slot]])
            if ci == 3:
                kb.dma("sp", gT_d[h].rearrange("p (a t) -> p a t", a=4)[:, :, gq * 512:(gq + 1) * 512], gTh[slot][:], b_gTh[slot], reads=[b_gTh[slot]], partial=[b_gT_d[h]])

        def gn_tail(c, h=h):
            p2 = c % 2
            ob = 5 + p2
            kb.op("dve", lambda e: e.reciprocal(out=mv[p2][:, 3:4], in_=mv[p2][:, 2:3]), reads=[b_mv2[p2]], writes=[b_mv3[p2]])
            kb.op("dve", lambda e: e.tensor_scalar(out=mv[p2][:, 2:3], in0=mv[p2][:, 0:1], scalar1=mv[p2][:, 3:4], scalar2=-1.0, op0=OP.mult, op1=OP.mult), reads=[b_mv[p2], b_mv3[p2]], writes=[b_mv2[p2]])
            kb.op("act", lambda e: e.activation(out=yn[p2][:], in_=psf(ob), func=AF.Identity, scale=mv[p2][:, 3:4], bias=mv[p2][:, 2:3]), reads=[PB[ob], b_mv2[p2], b_mv3[p2]], writes=[b_yn[p2]])
            kb.op("pool", lambda e: e.tensor_tensor(out=gat[p2][:], in0=yn[p2][:], in1=sgAll[:, c, :], op=OP.mult),
                  reads=[b_yn[p2], b_sgAll[c]], writes=[b_gat[p2]])

        for c in range(NT):
            cs = slice(c * 128, (c + 1) * 128)
            p2 = c % 2
            ob = 5 + p2
            for j2 in range(2):
                kb.op("pe", lambda e, j2=j2: e.matmul(ps_all[:, 2, 0:128], lhsT=krT2[:, j2, cs], rhs=QfT[:, j2, cs], start=(j2 == 0), stop=(j2 == 1)),
                      reads=[b_krT2, b_QfT], writes=[PBx["pt"]] if j2 == 0 else (), partial=[PBx["pt"]] if j2 else (), inc=(j2 == 1))
            kb.op("dve", lambda e: e.tensor_tensor(out=PTs[p2][:], in0=ps_all[:, 2, 0:128], in1=maskT[:, h, :], op=OP.mult), reads=[PBx["pt"], b_tabs], writes=[b_PTs[p2]])
            if c < NT - 1:
                if c == 0:
                    kt_scaled(0, zcol[:, h:h + 1], 0)
                state_update(c, p2, 0, g128[:, h:h + 1])
                if c + 1 < NT - 1:
                    kt_scaled(c + 1, zcol[:, h:h + 1], 1 - p2)
                kb.op("act", lambda e: e.activation(out=Sfb[1 - p2][:], in_=Sst[:], func=AF.Copy), reads=[b_Sst], writes=[b_Sfb[1 - p2]])
            kb.op("pe", lambda e: e.matmul(psf(ob), lhsT=PTs[p2][:], rhs=V2[:, c, :], start=True, stop=False), reads=[b_PTs[p2], b_V2], writes=[PB[ob]], inc=False)
            for j2 in range(2):
                kb.op("pe", lambda e, j2=j2: e.matmul(psf(ob), lhsT=QbT[:, j2, cs], rhs=SbAll[:, c, j2, :], start=False, stop=False), reads=[b_QbT, b_SbAll[c]], partial=[PB[ob]], inc=False)
            for j2 in range(2):
                kb.op("pe", lambda e, j2=j2: e.matmul(psf(ob), lhsT=QfT[:, j2, cs], rhs=Sfb[p2][:, j2, :], start=False, stop=(j2 == 1)), reads=[b_QfT, b_Sfb[p2]], partial=[PB[ob]], inc=(j2 == 1))
            if c > 0:
                gn_tail(c - 1)
            kb.op("dve", lambda e: e.bn_stats(out=st6[p2][:], in_=psf(ob)), reads=[PB[ob]], writes=[b_st6[p2]])
            kb.op("dve", lambda e: e.bn_aggr(out=mv[p2][:, 0:2], in_=st6[p2][:]), reads=[b_st6[p2]], writes=[b_mv[p2]])
            kb.op("act", lambda e: e.activation(out=mv[p2][:, 2:3], in_=mv[p2][:, 1:2], func=AF.Sqrt, bias=epsc[:]), reads=[b_mv[p2], b_eps], writes=[b_mv2[p2]])
            if c > 1:
                gated_T(c - 2)
        gn_tail(NT - 1)
        gated_T(NT - 2)
        gated_T(NT - 1)
    kb.barrier()
    ar.release(m5)
    ar.release(m_ret)
    if debug.get("dump_g"):
        stg = ar.alloc("stgg", [128, 16, T], BF16)
        b_stg = Buf("stgg")
        for h in range(H):
            kb.dma("sp", stg[:, h * 4:(h + 1) * 4, :], gT_d[h].rearrange("p (a t) -> p a t", a=4), b_stg, reads=[b_gT_d[h]], partial=[b_stg])
        dump("gT", stg[:], b_stg, [128, 16, T], BF16)
        kb.barrier()
    if stop <= 5:
        return finish()

    mT = ar.alloc("mT", [128, 8, T], BF16)
    m6 = ar.mark()
    zT6 = ar.alloc("zT6", [128, 8, T], BF16)
    gT6 = ar.alloc("gT6", [128, 16, T], BF16)
    b_mT = Buf("mT")
    b_zT6 = [Buf(f"zT6_{g}") for g in range(NG)]
    b_gT6 = [Buf(f"gT6_{g}") for g in range(NG)]
    for g in range(NG):
        gs = slice(g * 512, (g + 1) * 512)
        kb.dma("sp", zT6[:, :, gs], z_d.rearrange("p (c t) -> p c t", c=8)[:, :, gs], b_zT6[g], reads=[b_z_d], writes=[b_zT6[g]])
        for h in range(H):
            kb.dma("sp", gT6[:, h * 4:(h + 1) * 4, gs], gT_d[h].rearrange("p (a t) -> p a t", a=4)[:, :, gs], b_gT6[g], reads=[b_gT_d[h]],
                   writes=[b_gT6[g]] if h == 0 else (), partial=[b_gT6[g]] if h else ())
    wm = [ar.alloc(f"wm{i}", [128, 40, 128], BF16) for i in range(2)]
    b_wm = [Buf(f"wm{i}") for i in range(2)]
    sa = [ar.alloc(f"sa{i}", [128, 512], F32) for i in range(2)]
    sr = [ar.alloc(f"sr{i}", [128, 512], F32) for i in range(2)]
    t1 = [ar.alloc(f"t1{i}", [128, 512], F32) for i in range(2)]
    b_sa = [Buf(f"sa{i}") for i in range(2)]
    b_sr = [Buf(f"sr{i}") for i in range(2)]
    b_t1 = [Buf(f"t1{i}") for i in range(2)]
    w_a_v = w_a_out.rearrange("(c p) n -> p c n", p=128)
    w_r_v = w_r_out.rearrange("(c p) n -> p c n", p=128)

    def load_m_w(dcn):
        s = dcn % 2
        cs_ = slice(dcn * 128, (dcn + 1) * 128)
        kb.dma("pool", wm[s][:, 0:8, :], w_a_v[:, :, cs_], b_wm[s], reads=[b_in], writes=[b_wm[s]])
        kb.dma("pool", wm[s][:, 8:24, :], w_r_v[:, :, cs_], b_wm[s], reads=[b_in], partial=[b_wm[s]])
        kb.dma("pool", wm[s][:, 24:32, :], w_in_v[:, :, C_GA + dcn * 128: C_GA + (dcn + 1) * 128], b_wm[s], reads=[b_in], partial=[b_wm[s]])
        kb.dma("pool", wm[s][:, 32:40, :], w_in_v[:, :, C_GR + dcn * 128: C_GR + (dcn + 1) * 128], b_wm[s], reads=[b_in], partial=[b_wm[s]])

    def mm_group(bank, items, reads):
        n = len(items)
        for k, (l, r) in enumerate(items):
            kb.op("pe", lambda e, l=l, r=r, k=k: e.matmul(psf(bank), lhsT=l, rhs=r, start=(k == 0), stop=(k == n - 1)),
                  reads=reads, writes=[PB[bank]] if k == 0 else (), partial=[PB[bank]] if k else (), inc=(k == n - 1))

    load_m_w(0)
    it = 0
    for dcn in range(8):
        s = dcn % 2
        if dcn + 1 < 8:
            load_m_w(dcn + 1)
        for g in range(NG):
            gs = slice(g * 512, (g + 1) * 512)
            q = it % 2
            it += 1
            bk = (0, 1, 2, 3) if q == 0 else (4, 5, 6, 7)
            mm_group(bk[2], [(wm[s][:, 24 + dc, :], hT[:, dc, gs]) for dc in range(8)], [b_wm[s], b_hT])
            mm_group(bk[3], [(wm[s][:, 32 + dc, :], hT[:, dc, gs]) for dc in range(8)], [b_wm[s], b_hT])
            mm_group(bk[0], [(wm[s][:, c, :], zT6[:, c, gs]) for c in range(8)], [b_wm[s], b_zT6[g]])
            mm_group(bk[1], [(wm[s][:, 8 + c, :], gT6[:, c, gs]) for c in range(16)], [b_wm[s], b_gT6[g]])
            kb.op("act", lambda e: e.activation(out=sa[q][:], in_=psf(bk[2]), func=AF.Sigmoid), reads=[PB[bk[2]]], writes=[b_sa[q]])
            kb.op("act", lambda e: e.activation(out=sr[q][:], in_=psf(bk[3]), func=AF.Sigmoid), reads=[PB[bk[3]]], writes=[b_sr[q]])
            kb.op("dve", lambda e: e.tensor_tensor(out=t1[q][:], in0=psf(bk[0]), in1=sa[q][:], op=OP.mult), reads=[PB[bk[0]], b_sa[q]], writes=[b_t1[q]])
            kb.op("dve", lambda e: e.tensor_tensor(out=sr[q][:], in0=psf(bk[1]), in1=sr[q][:], op=OP.mult), reads=[PB[bk[1]], b_sr[q]], writes=[b_sr[q]])
            kb.op("dve", lambda e: e.tensor_tensor(out=mT[:, dcn, gs], in0=t1[q][:], in1=sr[q][:], op=OP.add), reads=[b_t1[q], b_sr[q]], partial=[b_mT])
    kb.barrier()
    if debug.get("dump_m"):
        dump("mT", mT[:], b_mT, [128, 8, T], BF16)
        kb.barrier()
    if stop <= 6:
        return finish()


    ar.release(m6)
    m7 = ar.mark()
    wo = ar.alloc("wo", [128, 8, D], BF16)
    b_wo = Buf("wo")
    w_o_v = w_o.rearrange("(c p) n -> p c n", p=128)
    kb.dma("pool", wo[:, :, 0:512], w_o_v[:, :, 0:512], b_wo, reads=[b_in], writes=[b_wo])
    kb.dma("pool", wo[:, :, 512:1024], w_o_v[:, :, 512:1024], b_wo, reads=[b_in], partial=[b_wo])
    xr = [ar.alloc(f"xr{i}", [128, D], F32) for i in range(2)]
    x1t = [ar.alloc(f"x1t{i}", [128, D], F32) for i in range(2)]
    b_xr = [Buf(f"xr{i}") for i in range(2)]
    b_x1t = [Buf(f"x1t{i}") for i in range(2)]
    junk7 = ar.alloc("junk7", [128, D], BF16)
    b_junk7 = Buf("junk7")
    xs7 = [ar.alloc(f"xs7{i}", [128, D], BF16) for i in range(2)]
    b_xs7 = [Buf(f"xs7{i}") for i in range(2)]
    st7 = [ar.alloc(f"st7{i}", [128, 4], F32) for i in range(2)]
    b_st7 = [Buf(f"st7{i}") for i in range(2)]
    kb.dma("sp", xr[0][:], x_seg[0:128, :], b_xr[0], reads=[b_in], writes=[b_xr[0]])
    pend7 = []
    for i in range(NT):
        q = i % 2
        if i + 1 < NT:
            kb.dma("sp", xr[1 - q][:], x_seg[(i + 1) * 128:(i + 2) * 128, :], b_xr[1 - q], reads=[b_in], writes=[b_xr[1 - q]])
        b0 = q * 2
        for half in range(2):
            mm_group(b0 + half, [(mT[:, dc, i * 128:(i + 1) * 128], wo[:, dc, half * 512:(half + 1) * 512]) for dc in range(8)], [b_mT, b_wo])
        kb.op("dve", lambda e: e.tensor_tensor(out=x1t[q][:].rearrange("p (a n) -> p a n", a=2), in0=ps_all[:, b0:b0 + 2, :], in1=xr[q][:].rearrange("p (a n) -> p a n", a=2), op=OP.add),
              reads=[PB[b0], PB[b0 + 1], b_xr[q]], writes=[b_x1t[q]])
        kb.dma("sp", x1_d[i * 128:(i + 1) * 128, :], x1t[q][:], b_x1t[q], reads=[b_x1t[q]], writes=[b_x1_d[i]])
        norm_to_T(x1t[q], b_x1t[q], 128, None, b_hT, gffnT, junk7, b_junk7, xs7[q], b_xs7[q], st7[q], b_st7[q], 4 + q, defer=True)
        pend7.append((lambda i=i: hT[:, :, i * 128:(i + 1) * 128], q))
        if len(pend7) > 1:
            fn_, q_ = pend7.pop(0)
            norm_B(128, fn_, b_hT, gffnT, xs7[q_], b_xs7[q_], 4 + q_)
    for fn_, q_ in pend7:
        norm_B(128, fn_, b_hT, gffnT, xs7[q_], b_xs7[q_], 4 + q_)
    kb.barrier()
    ar.release(m7)
    ar.release(m6)
    if debug.get("dump_h2"):
        dump("h2T", hT[:], b_hT, [128, 8, T], BF16)
        kb.barrier()
    if stop <= 7:
        return finish()

    ar.top = m6 - 8 * T * 2
    aT = ar.alloc("aT", [128, NFF, T], BF16)
    b_aT = Buf("aT")
    wd = ar.alloc("wd", [128, NFF, D], BF16)
    b_wd = Buf("wd")
    m8 = ar.mark()
    wgu = [ar.alloc(f"wgu{i}", [128, 8, 2, 128], BF16) for i in range(2)]
    b_wgu = [Buf(f"wgu{i}") for i in range(2)]
    sgf = [ar.alloc(f"sgf{i}", [128, 512], F32) for i in range(2)]
    b_sgf = [Buf(f"sgf{i}") for i in range(2)]
    w_g_v = w_gate.rearrange("(c p) n -> p c n", p=128)
    w_u_v = w_up.rearrange("(c p) n -> p c n", p=128)
    w_d_v = w_down.rearrange("(c p) n -> p c n", p=128)

    def load_gu(f):
        s = f % 2
        kb.dma("pool", wgu[s][:, :, 0, :], w_g_v[:, :, f * 128:(f + 1) * 128], b_wgu[s], reads=[b_in], writes=[b_wgu[s]])
        kb.dma("pool", wgu[s][:, :, 1, :], w_u_v[:, :, f * 128:(f + 1) * 128], b_wgu[s], reads=[b_in], partial=[b_wgu[s]])

    load_gu(0)
    it = 0
    for f in range(NFF):
        s = f % 2
        if debug.get("ffskip") and f >= 8:
            continue
        if f + 1 < NFF:
            load_gu(f + 1)
        if f in (1, 3, 5, 7):
            qd = (f - 1) // 2
            kb.dma("pool", wd[:, :, qd * 256:(qd + 1) * 256], w_d_v[:, :, qd * 256:(qd + 1) * 256], b_wd, reads=[b_in],
                   writes=[b_wd] if qd == 0 else (), partial=[b_wd] if qd else ())
        for g in range(NG):
            gs = slice(g * 512, (g + 1) * 512)
            q = it % 2
            it += 1
            bk = (0, 1) if q == 0 else (2, 3)
            mm_group(bk[0], [(wgu[s][:, dc, 0, :], hT[:, dc, gs]) for dc in range(8)], [b_wgu[s], b_hT])
            mm_group(bk[1], [(wgu[s][:, dc, 1, :], hT[:, dc, gs]) for dc in range(8)], [b_wgu[s], b_hT])
            kb.op("act", lambda e: e.activation(out=sgf[q][:], in_=psf(bk[0]), func=AF.Silu), reads=[PB[bk[0]]], writes=[b_sgf[q]])
            kb.op("dve", lambda e: e.tensor_tensor(out=aT[:, f, gs], in0=psf(bk[1]), in1=sgf[q][:], op=OP.mult), reads=[PB[bk[1]], b_sgf[q]], partial=[b_aT])
    kb.barrier()
    if debug.get("dump_a"):
        dump("aT", aT[:], b_aT, [128, NFF, T], BF16)
        kb.barrier()
    if stop <= 8:
        return finish()

    ar.release(m8)
    gfinb = ar.alloc("gfinb", [128, D], F32)
    b_gfinb = Buf("gfinb")
    kb.dma("sp", gfinb[:], gfin_in[0:1, :].partition_broadcast(128), b_gfinb, reads=[b_in], writes=[b_gfinb])
    x1r = [ar.alloc(f"x1r{i}", [128, D], F32) for i in range(2)]
    x2 = x1r
    ot = [ar.alloc(f"ot{i}", [128, D], F32) for i in range(2)]
    b_x1r = [Buf(f"x1r{i}") for i in range(2)]
    b_x2 = b_x1r
    b_ot = [Buf(f"ot{i}") for i in range(2)]
    junk9 = ar.alloc("junk9", [128, D], BF16)
    b_junk9 = Buf("junk9")
    st9 = [ar.alloc(f"st9{i}", [128, 4], F32) for i in range(2)]
    b_st9 = [Buf(f"st9{i}") for i in range(2)]
    kb.dma("sp", x1r[0][:], x1_d[0:128, :], b_x1r[0], reads=[b_x1_d[0]], writes=[b_x1r[0]])
    for i in range(NT):
        q = i % 2
        if i + 1 < NT:
            kb.dma("sp", x1r[1 - q][:], x1_d[(i + 1) * 128:(i + 2) * 128, :], b_x1r[1 - q], reads=[b_x1_d[i + 1]], writes=[b_x1r[1 - q]])
        b0 = q * 2
        for half in range(2):
            mm_group(b0 + half, [(aT[:, f, i * 128:(i + 1) * 128], wd[:, f, half * 512:(half + 1) * 512]) for f in range(NFF)], [b_aT, b_wd])
        kb.op("dve", lambda e: e.tensor_tensor(out=x2[q][:].rearrange("p (a n) -> p a n", a=2), in0=ps_all[:, b0:b0 + 2, :], in1=x1r[q][:].rearrange("p (a n) -> p a n", a=2), op=OP.add),
              reads=[PB[b0], PB[b0 + 1]], writes=[b_x2[q]])
        kb.op("act", lambda e: e.activation(out=junk9[:], in_=x2[q][:], func=AF.Square, accum_out=st9[q][:, 0:1]), reads=[b_x2[q]], writes=[b_junk9, b_st9[q]], multi=True)
        kb.op("act", lambda e: e.activation(out=st9[q][:, 1:2], in_=st9[q][:, 0:1], func=AF.Sqrt, scale=1.0 / D, bias=epsc[:]), reads=[b_st9[q], b_eps], writes=[b_st9[q]])
        kb.op("dve", lambda e: e.reciprocal(out=st9[q][:, 2:3], in_=st9[q][:, 1:2]), reads=[b_st9[q]], writes=[b_st9[q]])
        kb.op("dve", lambda e: e.scalar_tensor_tensor(out=ot[q][:], in0=x2[q][:], scalar=st9[q][:, 2:3], in1=gfinb[:], op0=OP.mult, op1=OP.mult),
              reads=[b_x2[q], b_st9[q], b_gfinb], writes=[b_ot[q]])
        kb.dma("sp", y_out[i * 128:(i + 1) * 128, :], ot[q][:], b_ot[q], reads=[b_ot[q]])
    return finish()


_FREQ = (10000.0 ** (-(np.arange(0, DK, 2, dtype=np.float32) / np.float32(DK)))).astype(np.float32).reshape(128, 1)


def make_in_maps(x, g_mix, w_in, w_conv, dec_f, dec_b, g_ret, w_a_out, w_r_out, w_o,
                 g_ffn, w_ff_gate, w_ff_up, w_ff_down, g_final):
    f = np.float32
    c_ = np.ascontiguousarray
    shared = {
        "freq": _FREQ,
        "dec": c_(np.concatenate([np.asarray(dec_f, f).reshape(1, 4), np.asarray(dec_b, f).reshape(1, 4)], axis=1)),
        "g_mixT": c_(np.asarray(g_mix, f).reshape(8, 128).T),
        "g_ffnT": c_(np.asarray(g_ffn, f).reshape(8, 128).T),
        "w_convT": c_(np.asarray(w_conv, f).reshape(3, 8, 128).transpose(2, 1, 0).reshape(128, 24)),
        "g_ret": c_(np.asarray(g_ret, f).reshape(1, H * DV)),
        "g_final": c_(np.asarray(g_final, f).reshape(1, D)),
        "w_in": c_(np.asarray(w_in, f).reshape(D, NIN)),
        "w_a_out": c_(np.asarray(w_a_out, f).reshape(D, D)),
        "w_r_out": c_(np.asarray(w_r_out, f).reshape(H * DV, D)),
        "w_o": c_(np.asarray(w_o, f).reshape(D, D)),
        "w_ff_gate": c_(np.asarray(w_ff_gate, f).reshape(D, DFF)),
        "w_ff_up": c_(np.asarray(w_ff_up, f).reshape(D, DFF)),
        "w_ff_down": c_(np.asarray(w_ff_down, f).reshape(DFF, D)),
    }
    x = np.asarray(x, f)
    in_maps = []
    for c in range(8):
        b, s = divmod(c, NSEG)
        halo = np.zeros((2, D), f)
        if s > 0:
            halo[0] = x[b, s * T - 1]
        if s < NSEG - 1:
            halo[1] = x[b, (s + 1) * T]
        segw = np.zeros((1, 16), f)
        for r in range(NSEG):
            if r < s:
                segw[0, r] = T * (s - 1 - r)
                segw[0, 4 + r] = 1.0
            if r > s:
                segw[0, 8 + r] = T * (r - s - 1)
                segw[0, 12 + r] = 1.0
        m = dict(shared)
        m["x_seg"] = c_(x[b, s * T:(s + 1) * T])
        m["x_halo"] = halo
        m["pos"] = np.arange(s * T, (s + 1) * T, dtype=f).reshape(1, T)
        m["segw"] = segw
        in_maps.append(m)
    return in_maps


def kernel(x, g_mix, w_in, w_conv, dec_f, dec_b, g_ret, w_a_out, w_r_out, w_o,
           g_ffn, w_ff_gate, w_ff_up, w_ff_down, g_final):
    nc, dumps, kb, ar = build_program()
    in_maps = make_in_maps(x, g_mix, w_in, w_conv, dec_f, dec_b, g_ret, w_a_out, w_r_out, w_o,
                           g_ffn, w_ff_gate, w_ff_up, w_ff_down, g_final)
    res = run_bass_kernel_spmd(nc, in_maps, core_ids=list(range(8)))
    out = np.empty((2, SEQ, D), np.float32)
    for c in range(8):
        b, s = divmod(c, NSEG)
        out[b, s * T:(s + 1) * T] = np.asarray(res.results[c]["y"], np.float32)
    return out
```

```python
import math
import contextlib
import numpy as np
import concourse.bass as bass
import concourse.mybir as mybir
from concourse.bass_utils import run_bass_kernel_spmd

F32 = mybir.dt.float32
BF16 = mybir.dt.bfloat16
I32 = mybir.dt.int32
U8 = mybir.dt.uint8
AF = mybir.ActivationFunctionType
OP = mybir.AluOpType

D = 1024
SEQ = 8192
NSEG = 4
T = SEQ // NSEG
NT = T // 128
NG = T // 512
H = 4
DK = 256
DV = 512
DFF = 2816
NFF = DFF // 128
NIN = 11264
EPS = 1e-6
C_XC, C_GB, C_GC, C_Q, C_K, C_V, C_GSW, C_GA, C_GR = 0, 1024, 2048, 3072, 4096, 5120, 7168, 9216, 10240
ARENA_BYTES = 206 * 1024
SAME_ENGINE_SYNC = True
TWO_PI_HI = 6.28125
TWO_PI_LO = 2.0 * math.pi - 6.28125


def _dsize(dt):
    return {F32: 4, BF16: 2, I32: 4, U8: 1}[dt]


class Buf:
    __slots__ = ("name", "w", "r", "sem", "dcnt")

    def __init__(self, name):
        self.name = name
        self.w = {}
        self.r = {}
        self.sem = None
        self.dcnt = 0


class KB:
    def __init__(self, nc):
        self.nc = nc
        self.es = contextlib.ExitStack()
        self.eng = {"pe": nc.tensor, "act": nc.scalar, "dve": nc.vector, "pool": nc.gpsimd, "sp": nc.sync}
        self.sems = {}
        self.cnt = {}
        for e in ("pe", "act", "dve", "pool"):
            self.sems[e] = self.es.enter_context(nc.semaphore("s_" + e))
            self.cnt[e] = 0
        self.seen = {e: {} for e in self.eng}
        self.snap = {}
        self.all = {}
        self.dbufs = {}
        self.nwaits = 0
        self.nops = 0

    def _cur(self, k):
        if k in self.cnt:
            return self.cnt[k]
        return self.dbufs[k].dcnt

    def _emit_waits(self, e, reads, writes, partial, attach=False):
        need = {}
        todo = []

        def add(d):
            for k, v in d.items():
                if need.get(k, 0) < v:
                    need[k] = v

        for b in reads:
            add(b.w)
        for b in writes:
            add(b.w)
            add(b.r)
        for b in partial:
            add(b.r)
        seen = self.seen[e]
        eng = self.eng[e]
        for k, v in need.items():
            if k == e and (e == "pe" or not SAME_ENGINE_SYNC):
                continue
            if k in self.dbufs:
                v = max(v, self.dbufs[k].dcnt)
            if seen.get(k, 0) >= v:
                continue
            assert v <= self._cur(k), ("wait on un-issued event", e, k, v, self._cur(k))
            todo.append((k, v))
            seen[k] = v
            sn = self.snap.get((k, v))
            if sn:
                for k2, v2 in sn.items():
                    if seen.get(k2, 0) < v2:
                        seen[k2] = v2
            self.nwaits += 1
        keep = todo[-1:] if (attach and todo) else []
        for k, v in (todo[:-1] if (attach and todo) else todo):
            eng.wait_ge(self.sems[k], v)
        return keep

    def op(self, e, fn, reads=(), writes=(), partial=(), inc=True, multi=False):
        keep = self._emit_waits(e, reads, writes, partial, attach=not multi)
        ins = fn(self.eng[e])
        for k, v in keep:
            ins._wait_ge(self.sems[k], v)
        self.nops += 1
        if inc:
            self.cnt[e] += 1
            ins.then_inc(self.sems[e], 1)
            val = self.cnt[e]
            self.all[e] = val
            self.snap[(e, val)] = dict(self.seen[e])
        else:
            val = self.cnt[e] + 1
        for b in reads:
            if b.r.get(e, 0) < val:
                b.r[e] = val
        for b in writes:
            b.w = {e: val}
            b.r = {}
        for b in partial:
            if b.w.get(e, 0) < val:
                b.w[e] = val
        return ins

    def dma(self, q, out_ap, in_ap, sb, reads=(), writes=(), partial=(), **kw):
        if sb.sem is None:
            sb.sem = self.es.enter_context(self.nc.semaphore("d_" + sb.name))
            key = "d:" + sb.name
            assert key not in self.sems, key
            self.sems[key] = sb.sem
            self.dbufs[key] = sb
        key = "d:" + sb.name
        keep = self._emit_waits(q, reads, writes, partial, attach=True)
        ins = self.eng[q].dma_start(out=out_ap, in_=in_ap, **kw)
        for k, v in keep:
            ins._wait_ge(self.sems[k], v)
        sb.dcnt += 16
        ins.then_inc(sb.sem, 16)
        val = sb.dcnt
        self.all[key] = val
        self.snap[(key, val)] = dict(self.seen[q])
        self.nops += 1
        for b in reads:
            if b.r.get(key, 0) < val:
                b.r[key] = val
        for b in writes:
            b.w = {key: val}
            b.r = {}
        for b in partial:
            if b.w.get(key, 0) < val:
                b.w[key] = val
        return ins

    def barrier(self, engines=("pe", "act", "dve", "pool", "sp")):
        for e in engines:
            seen = self.seen[e]
            eng = self.eng[e]
            for k, v in self.all.items():
                if k == "cc":
                    continue
                if k == e and (e == "pe" or not SAME_ENGINE_SYNC):
                    continue
                if seen.get(k, 0) >= v:
                    continue
                eng.wait_ge(self.sems[k], v)
                seen[k] = v
                sn = self.snap.get((k, v))
                if sn:
                    for k2, v2 in sn.items():
                        if seen.get(k2, 0) < v2:
                            seen[k2] = v2
                self.nwaits += 1


class Arena:
    def __init__(self, nc, nbytes):
        self.nc = nc
        self.t = nc.alloc_sbuf_tensor("arena", [128, nbytes], U8)
        self.base = nc.lookup_mloc(self.t).addr
        self.nbytes = nbytes
        self.top = 0
        self.n = 0
        self.peak = 0

    def alloc(self, name, shape, dtype):
        size = _dsize(dtype)
        for s in shape[1:]:
            size *= s
        size = (size + 31) // 32 * 32
        off = self.base + self.top
        self.top += size
        self.peak = max(self.peak, self.top)
        assert self.top <= self.nbytes, ("arena overflow", name, self.top)
        self.n += 1
        return self.nc.alloc_sbuf_tensor_at(f"{name}_{self.n}", list(shape), dtype, offset=off)

    def mark(self):
        return self.top

    def release(self, m):
        self.top = m


def build_program(debug=None):
    debug = debug or {}
    stop = debug.get("stop", 99)
    nc = bass.Bass("TRN2", target_bir_lowering=False)
    kb = KB(nc)
    ar = Arena(nc, ARENA_BYTES)
    dumps = []

    def dram_in(name, shape, dt=F32):
        return nc.dram_tensor(name, list(shape), dt, kind="ExternalInput").ap()

    x_seg = dram_in("x_seg", [T, D])
    x_halo = dram_in("x_halo", [2, D])
    pos_in = dram_in("pos", [1, T])
    freq_in = dram_in("freq", [128, 1])
    segw_in = dram_in("segw", [1, 16])
    dec_in = dram_in("dec", [1, 8])
    gmixT_in = dram_in("g_mixT", [128, 8])
    gffnT_in = dram_in("g_ffnT", [128, 8])
    wconvT_in = dram_in("w_convT", [128, 24])
    gret_in = dram_in("g_ret", [1, H * DV])
    gfin_in = dram_in("g_final", [1, D])
    w_in = dram_in("w_in", [D, NIN])
    w_a_out = dram_in("w_a_out", [D, D])
    w_r_out = dram_in("w_r_out", [H * DV, D])
    w_o = dram_in("w_o", [D, D])
    w_gate = dram_in("w_ff_gate", [D, DFF])
    w_up = dram_in("w_ff_up", [D, DFF])
    w_down = dram_in("w_ff_down", [DFF, D])
    y_out = nc.dram_tensor("y", [T, D], F32, kind="ExternalOutput").ap()

    kT_d = nc.dram_tensor("kT_d", [H, 128, 2 * T], BF16, kind="Internal").ap()
    V_d = nc.dram_tensor("V_d", [H, 128, NT * DV], BF16, kind="Internal").ap()
    z_d = nc.dram_tensor("z_d", [128, 8 * T], BF16, kind="Internal").ap()
    gT_d = nc.dram_tensor("gT_d", [H, 128, 4 * T], BF16, kind="Internal").ap()
    x1_d = nc.dram_tensor("x1_d", [T, D], F32, kind="Internal").ap()
    cc_src = [nc.dram_tensor(f"cc_src{h}", [2 * 2 * 128, DV], F32, kind="Internal").ap() for h in range(H)]
    cc_dst = [nc.dram_tensor(f"cc_dst{h}", [NSEG * 2 * 2 * 128, DV], F32, kind="Internal").ap() for h in range(H)]
    sin_d = nc.dram_tensor("sin_d", [H * 2, 128, 2 * DV], F32, kind="Internal").ap()
    b_sin_d = [Buf(f"sin_d{i}") for i in range(H * 2)]
    b_kT_d = [Buf(f"kT_d{h}") for h in range(H)]
    b_V_d = [Buf(f"V_d{h}") for h in range(H)]
    b_z_d = Buf("z_d")
    b_gT_d = [Buf(f"gT_d{h}") for h in range(H)]
    b_x1_d = [Buf(f"x1_d{i}") for i in range(NT)]
    b_cc_src = [Buf(f"cc_src{h}") for h in range(H)]
    b_cc_dst = [Buf(f"cc_dst{h}") for h in range(H)]
    cc_sem = kb.es.enter_context(nc.semaphore("cc"))
    kb.sems["cc"] = cc_sem
    kb.cnt["cc"] = 0
    b_in = Buf("inputs")

    w_in_v = w_in.rearrange("(c p) n -> p c n", p=128)

    ps_all = nc.alloc_psum_tensor("ps_all", [128, 8, 512], F32)
    PB = [Buf(f"pb{i}") for i in range(8)]

    def psf(i, n=512):
        return ps_all[:, i, 0:n]

    def psb(i):
        return ps_all[:, i, :].bitcast(BF16)

    def dump(name, ap, buf, shape, dt):
        t = nc.dram_tensor("dbg_" + name, list(shape), dt, kind="ExternalOutput").ap()
        kb.dma("sp", t, ap, buf, reads=[buf])
        dumps.append("dbg_" + name)

    def finish():
        kb.barrier(engines=("sp",))
        return nc, dumps, kb, ar

    cbuf = Buf("consts")
    ident = ar.alloc("ident", [128, 128], BF16)
    gmixT = ar.alloc("gmixT", [128, 8], F32)
    gffnT = ar.alloc("gffnT", [128, 8], F32)
    wconvT = ar.alloc("wconvT", [128, 24], F32)
    dec = ar.alloc("dec", [128, 8], F32)
    segw = ar.alloc("segw", [128, 16], F32)
    freq = ar.alloc("freq", [128, 1], F32)
    lg = ar.alloc("lg", [128, 8], F32)
    g128 = ar.alloc("g128", [128, 8], F32)
    zcol = ar.alloc("zcol", [128, 8], F32)
    zp = ar.alloc("zp", [128, 8, NT], F32)
    wseg = ar.alloc("wseg", [128, 8, 4], F32)
    maskT = ar.alloc("maskT", [128, H, 128], F32)
    xif = ar.alloc("xif", [128, H, 128], F32)
    xib = ar.alloc("xib", [128, H, 128], F32)
    b_ident, b_lg, b_tabs = Buf("ident"), Buf("lg"), Buf("tabs")
    for dst, src in ((gmixT, gmixT_in), (gffnT, gffnT_in), (wconvT, wconvT_in), (freq, freq_in)):
        kb.dma("sp", dst[:], src, cbuf, reads=[b_in], partial=[cbuf])
    kb.dma("sp", dec[:], dec_in[0:1, :].partition_broadcast(128), cbuf, reads=[b_in], partial=[cbuf])
    kb.dma("sp", segw[:], segw_in[0:1, :].partition_broadcast(128), cbuf, reads=[b_in], partial=[cbuf])

    m_setup = ar.mark()
    it_i = ar.alloc("it_i", [128, 128], I32)
    jmi = ar.alloc("jmi", [128, 128], F32)
    fi1 = ar.alloc("fi1", [128, 128], F32)
    fr = ar.alloc("fr", [128, 128], F32)
    pp1 = ar.alloc("pp1", [128, 1], F32)
    pr = ar.alloc("pr", [128, 1], F32)
    pi_ = ar.alloc("pi_", [128, 1], F32)
    tpos = ar.alloc("tpos", [128, NT], F32)
    rpos = ar.alloc("rpos", [128, NT], F32)
    tmpA = ar.alloc("tmpA", [128, 128], F32)
    tmpB = ar.alloc("tmpB", [128, 128], F32)
    tmpC = ar.alloc("tmpC", [128, 128], F32)
    tmpM = ar.alloc("tmpM", [128, 128], F32)
    s8 = [ar.alloc(f"s8_{i}", [128, 8], F32) for i in range(6)]
    b_it, b_set = Buf("it"), Buf("setup")
    bt = [Buf(f"tmp{i}") for i in range(12)]

    def iota_f32(dst, dstbuf, pattern, base, cm, n):
        kb.op("pool", lambda e: e.iota(it_i[:, 0:n], pattern=pattern, base=base, channel_multiplier=cm), writes=[b_it])
        kb.op("dve", lambda e: e.tensor_copy(out=dst, in_=it_i[:, 0:n]), reads=[b_it], partial=[dstbuf])

    iota_f32(jmi[:], b_set, [[-1, 128]], 0, 1, 128)
    kb.op("dve", lambda e: e.tensor_single_scalar(out=tmpA[:], in_=jmi[:], scalar=0.0, op=OP.is_equal), reads=[b_set], writes=[bt[0]])
    kb.op("dve", lambda e: e.tensor_copy(out=ident[:], in_=tmpA[:]), reads=[bt[0]], writes=[b_ident])
    iota_f32(fi1[:], b_set, [[1, 128]], 1, 0, 128)
    iota_f32(fr[:], b_set, [[-1, 128]], 128, 0, 128)
    iota_f32(pp1[:], b_set, [[0, 1]], 1, 1, 1)
    iota_f32(pr[:], b_set, [[0, 1]], 127, -1, 1)
    iota_f32(pi_[:], b_set, [[0, 1]], 0, 1, 1)
    iota_f32(tpos[:], b_set, [[128, NT]], 0, 1, NT)
    iota_f32(rpos[:], b_set, [[-128, NT]], T - 1, -1, NT)

    absd, u_, s_, s2, pl, mn = s8
    bl = [Buf(f"l{i}") for i in range(8)]
    kb.op("dve", lambda e: e.tensor_scalar(out=mn[:], in0=dec[:], scalar1=-1.0, scalar2=None, op0=OP.mult), reads=[cbuf], writes=[bl[5]])
    kb.op("dve", lambda e: e.tensor_tensor(out=absd[:], in0=dec[:], in1=mn[:], op=OP.max), reads=[cbuf, bl[5]], writes=[bl[0]])
    kb.op("act", lambda e: e.activation(out=u_[:], in_=absd[:], func=AF.Exp, scale=-1.0), reads=[bl[0]], writes=[bl[1]])
    kb.op("dve", lambda e: e.tensor_scalar(out=s_[:], in0=u_[:], scalar1=2.0, scalar2=None, op0=OP.add), reads=[bl[1]], writes=[bl[2]])
    kb.op("dve", lambda e: e.reciprocal(out=s_[:], in_=s_[:]), reads=[bl[2]], writes=[bl[2]])
    kb.op("dve", lambda e: e.tensor_tensor(out=s_[:], in0=s_[:], in1=u_[:], op=OP.mult), reads=[bl[1], bl[2]], writes=[bl[2]])
    kb.op("dve", lambda e: e.tensor_tensor(out=s2[:], in0=s_[:], in1=s_[:], op=OP.mult), reads=[bl[2]], writes=[bl[3]])
    kb.op("dve", lambda e: e.tensor_scalar(out=pl[:], in0=s2[:], scalar1=1.0 / 11, scalar2=1.0 / 9, op0=OP.mult, op1=OP.add), reads=[bl[3]], writes=[bl[4]])
    for cst in (1.0 / 7, 1.0 / 5, 1.0 / 3, 1.0):
        kb.op("dve", lambda e: e.tensor_tensor(out=pl[:], in0=pl[:], in1=s2[:], op=OP.mult), reads=[bl[3], bl[4]], writes=[bl[4]])
        kb.op("dve", lambda e, cst=cst: e.tensor_scalar(out=pl[:], in0=pl[:], scalar1=cst, scalar2=None, op0=OP.add), reads=[bl[4]], writes=[bl[4]])
    kb.op("dve", lambda e: e.tensor_tensor(out=pl[:], in0=pl[:], in1=s_[:], op=OP.mult), reads=[bl[2], bl[4]], writes=[bl[4]])
    kb.op("dve", lambda e: e.tensor_scalar(out=mn[:], in0=dec[:], scalar1=0.0, scalar2=None, op0=OP.min), reads=[cbuf], writes=[bl[5]])
    kb.op("dve", lambda e: e.scalar_tensor_tensor(out=lg[:], in0=pl[:], scalar=-2.0, in1=mn[:], op0=OP.mult, op1=OP.add), reads=[bl[4], bl[5]], writes=[b_lg])
    kb.op("act", lambda e: e.activation(out=g128[:], in_=lg[:], func=AF.Exp, scale=128.0), reads=[b_lg], partial=[b_tabs])
    LN_SCALE = math.log(DK ** -0.5)
    lnsc = ar.alloc("lnsc", [128, 1], F32)
    kb.op("dve", lambda e: e.memset(lnsc[:], LN_SCALE), partial=[b_set])
    for h in range(H):
        lgf = lg[:, h:h + 1]
        lgb = lg[:, 4 + h:5 + h]
        kb.op("act", lambda e, h=h, lgf=lgf: e.activation(out=xif[:, h, :], in_=fi1[:], func=AF.Exp, scale=lgf, bias=lnsc[:]), reads=[b_lg, b_set], partial=[b_tabs])
        kb.op("act", lambda e, h=h, lgb=lgb: e.activation(out=xib[:, h, :], in_=fr[:], func=AF.Exp, scale=lgb, bias=lnsc[:]), reads=[b_lg, b_set], partial=[b_tabs])
        kb.op("act", lambda e, h=h, lgf=lgf: e.activation(out=zcol[:, h:h + 1], in_=pr[:], func=AF.Exp, scale=lgf), reads=[b_lg, b_set], partial=[b_tabs])
        kb.op("act", lambda e, h=h, lgb=lgb: e.activation(out=zcol[:, 4 + h:5 + h], in_=pi_[:], func=AF.Exp, scale=lgb), reads=[b_lg, b_set], partial=[b_tabs])
        kb.op("act", lambda e, h=h, lgf=lgf: e.activation(out=zp[:, h, :], in_=rpos[:], func=AF.Exp, scale=lgf), reads=[b_lg, b_set], partial=[b_tabs])
        kb.op("act", lambda e, h=h, lgb=lgb: e.activation(out=zp[:, 4 + h, :], in_=tpos[:], func=AF.Exp, scale=lgb), reads=[b_lg, b_set], partial=[b_tabs])
        kb.op("act", lambda e, h=h, lgf=lgf: e.activation(out=wseg[:, h, :], in_=segw[:, 0:4], func=AF.Exp, scale=lgf), reads=[b_lg, cbuf], writes=[bt[1]])
        kb.op("dve", lambda e, h=h: e.tensor_tensor(out=wseg[:, h, :], in0=wseg[:, h, :], in1=segw[:, 4:8], op=OP.mult), reads=[bt[1], cbuf], partial=[b_tabs])
        kb.op("act", lambda e, h=h, lgb=lgb: e.activation(out=wseg[:, 4 + h, :], in_=segw[:, 8:12], func=AF.Exp, scale=lgb), reads=[b_lg, cbuf], writes=[bt[2]])
        kb.op("dve", lambda e, h=h: e.tensor_tensor(out=wseg[:, 4 + h, :], in0=wseg[:, 4 + h, :], in1=segw[:, 12:16], op=OP.mult), reads=[bt[2], cbuf], partial=[b_tabs])
        kb.op("dve", lambda e, lgb=lgb: e.tensor_scalar(out=tmpA[:], in0=jmi[:], scalar1=lgb, scalar2=None, op0=OP.mult), reads=[b_lg, b_set], writes=[bt[3]])
        kb.op("dve", lambda e, lgf=lgf: e.tensor_scalar(out=tmpB[:], in0=fi1[:], scalar1=lgf, scalar2=None, op0=OP.mult), reads=[b_lg, b_set], writes=[bt[4]])
        kb.op("dve", lambda e: e.tensor_tensor(out=tmpA[:], in0=tmpA[:], in1=tmpB[:], op=OP.subtract), reads=[bt[3], bt[4]], writes=[bt[3]])
        kb.op("dve", lambda e, lgf=lgf: e.tensor_scalar(out=tmpC[:], in0=pp1[:].to_broadcast([128, 128]), scalar1=lgf, scalar2=-1.0, op0=OP.mult, op1=OP.mult), reads=[b_lg, b_set], writes=[bt[5]])
        kb.op("dve", lambda e: e.tensor_tensor(out=tmpC[:], in0=tmpC[:], in1=tmpA[:], op=OP.subtract), reads=[bt[5], bt[3]], writes=[bt[5]])
        kb.op("dve", lambda e: e.tensor_single_scalar(out=tmpM[:], in_=jmi[:], scalar=0.0, op=OP.is_le), reads=[b_set], writes=[bt[6]])
        kb.op("dve", lambda e: e.tensor_tensor(out=tmpC[:], in0=tmpC[:], in1=tmpM[:], op=OP.mult), reads=[bt[5], bt[6]], writes=[bt[5]])
        kb.op("dve", lambda e: e.tensor_tensor(out=tmpA[:], in0=tmpA[:], in1=tmpC[:], op=OP.add), reads=[bt[3], bt[5]], writes=[bt[3]])
        kb.op("act", lambda e, h=h: e.activation(out=maskT[:, h, :], in_=tmpA[:], func=AF.Exp), reads=[bt[3]], partial=[b_tabs])
    kb.barrier()
    ar.release(m_setup)

    if debug.get("dump_consts"):
        dump("lg", lg[:], b_lg, [128, 8], F32)
        dump("maskT", maskT[:], b_tabs, [128, H, 128], F32)
        dump("xif", xif[:], b_tabs, [128, H, 128], F32)
        dump("xib", xib[:], b_tabs, [128, H, 128], F32)
        dump("zp", zp[:], b_tabs, [128, 8, NT], F32)
        dump("zcol", zcol[:], b_tabs, [128, 8], F32)
        dump("wseg", wseg[:], b_tabs, [128, 8, 4], F32)
        dump("g128", g128[:], b_tabs, [128, 8], F32)
    if stop <= 0:
        return finish()

    hT = ar.alloc("hT", [128, 8, T], BF16)
    hTh = ar.alloc("hTh", [128, 8, 2], BF16)
    b_hT = Buf("hT")
    b_hTh = Buf("hTh")

    def norm_to_T(xt, b_xt, np_, dstT_ap_fn, b_dst, gT, junk, b_junk, xs, b_xs, st, b_st, pbank, defer=False):
        kb.op("act", lambda e: e.activation(out=junk[0:np_, :], in_=xt[0:np_, :], func=AF.Square, accum_out=st[0:np_, 0:1]), reads=[b_xt], writes=[b_junk, b_st], multi=True)
        kb.op("act", lambda e: e.activation(out=st[0:np_, 1:2], in_=st[0:np_, 0:1], func=AF.Sqrt, scale=1.0 / D, bias=epsc[0:np_, :]), reads=[b_st], writes=[b_st])
        kb.op("dve", lambda e: e.reciprocal(out=st[0:np_, 2:3], in_=st[0:np_, 1:2]), reads=[b_st], writes=[b_st])
        kb.op("dve", lambda e: e.tensor_scalar(out=xs[0:np_, :], in0=xt[0:np_, :], scalar1=st[0:np_, 2:3], scalar2=None, op0=OP.mult), reads=[b_xt, b_st], writes=[b_xs])
        if not defer:
            norm_B(np_, dstT_ap_fn, b_dst, gT, xs, b_xs, pbank)

    def norm_B(np_, dstT_ap_fn, b_dst, gT, xs, b_xs, pbank):
        pv = psb(pbank)[:, 0:8 * np_].rearrange("p (c t) -> p c t", c=8)
        for c in range(8):
            kb.op("pe", lambda e, c=c: e.transpose(out=pv[:, c, :], in_=xs[0:np_, c * 128:(c + 1) * 128], identity=ident[0:np_, 0:np_]),
                  reads=[b_xs, b_ident], writes=[PB[pbank]] if c == 0 else (), partial=[PB[pbank]] if c else (), inc=(c == 7))
        kb.op("dve", lambda e: e.tensor_tensor(out=dstT_ap_fn(), in0=pv, in1=gT[:, :].unsqueeze(2).to_broadcast([128, 8, np_]), op=OP.mult),
              reads=[PB[pbank], cbuf], partial=[b_dst])

    epsc = ar.alloc("epsc", [128, 1], F32)
    b_eps = Buf("eps")
    kb.op("dve", lambda e: e.memset(epsc[:], EPS), writes=[b_eps])
    kb.barrier(engines=("act", "dve"))

    m1 = ar.mark()
    NXB = 3
    xts = [ar.alloc(f"xt{i}", [128, D], F32) for i in range(NXB)]
    b_xts = [Buf(f"xt{i}") for i in range(NXB)]
    junk = ar.alloc("junk", [128, D], BF16)
    b_junk = Buf("junk")
    xss = [ar.alloc(f"xs{i}", [128, D], BF16) for i in range(2)]
    b_xss = [Buf(f"xs{i}") for i in range(2)]
    sts = [ar.alloc(f"st{i}", [128, 4], F32) for i in range(2)]
    b_sts = [Buf(f"st{i}") for i in range(2)]
    kb.dma("sp", xts[0][0:2, :], x_halo, b_xts[0], reads=[b_in], writes=[b_xts[0]])
    norm_to_T(xts[0], b_xts[0], 2, lambda: hTh[:, :, :], b_hTh, gmixT, junk, b_junk, xss[0], b_xss[0], sts[0], b_sts[0], 0)
    for i in range(min(NXB - 1, NT)):
        kb.dma("sp", xts[(i + 1) % NXB][:], x_seg[i * 128:(i + 1) * 128, :], b_xts[(i + 1) % NXB], reads=[b_in], writes=[b_xts[(i + 1) % NXB]])
    pend1 = []
    for i in range(NT):
        j = i + NXB - 1
        if j < NT:
            kb.dma("sp", xts[(j + 1) % NXB][:], x_seg[j * 128:(j + 1) * 128, :], b_xts[(j + 1) % NXB], reads=[b_in], writes=[b_xts[(j + 1) % NXB]])
        s = (i + 1) % NXB
        norm_to_T(xts[s], b_xts[s], 128, None, b_hT, gmixT, junk, b_junk,
                  xss[i % 2], b_xss[i % 2], sts[i % 2], b_sts[i % 2], i % 2, defer=True)
        pend1.append((lambda i=i: hT[:, :, i * 128:(i + 1) * 128], i % 2))
        if len(pend1) > 1:
            fn_, q_ = pend1.pop(0)
            norm_B(128, fn_, b_hT, gmixT, xss[q_], b_xss[q_], q_)
    for fn_, q_ in pend1:
        norm_B(128, fn_, b_hT, gmixT, xss[q_], b_xss[q_], q_)
    kb.barrier()
    ar.release(m1)
    if debug.get("dump_hT"):
        dump("hT", hT[:], b_hT, [128, 8, T], BF16)
        dump("hTh", hTh[:], b_hTh, [128, 8, 2], BF16)
    if stop <= 1:
        return finish()

    m_ret = ar.mark()
    cosT = ar.alloc("cosT", [128, T], F32)
    sinT = ar.alloc("sinT", [128, T], F32)
    b_rot = Buf("rot")
    m_r0 = ar.mark()
    ang = ar.alloc("ang", [128, T], F32)
    kk = ar.alloc("kk", [128, T], F32)
    ki = ar.alloc("ki", [128, T], I32)
    b_ang, b_kk, b_ki = Buf("ang"), Buf("kk"), Buf("ki")
    kb.dma("sp", ang[:], pos_in[0:1, :].partition_broadcast(128), b_ang, reads=[b_in], writes=[b_ang])
    kb.op("dve", lambda e: e.tensor_scalar(out=ang[:], in0=ang[:], scalar1=freq[:, 0:1], scalar2=None, op0=OP.mult), reads=[b_ang, cbuf], writes=[b_ang])
    hpi = ar.alloc("hpi", [128, 1], F32)
    b_hpi = Buf("hpi")
    kb.op("dve", lambda e: e.memset(hpi[:], math.pi / 2), writes=[b_hpi])
    kb.op("dve", lambda e: e.tensor_scalar(out=kk[:], in0=ang[:], scalar1=1.0 / (2 * math.pi), scalar2=None, op0=OP.mult), reads=[b_ang], writes=[b_kk])
    kb.op("dve", lambda e: e.tensor_copy(out=ki[:], in_=kk[:]), reads=[b_kk], writes=[b_ki])
    kb.op("dve", lambda e: e.tensor_copy(out=kk[:], in_=ki[:]), reads=[b_ki], writes=[b_kk])
    kb.op("dve", lambda e: e.scalar_tensor_tensor(out=sinT[:], in0=kk[:], scalar=-TWO_PI_HI, in1=ang[:], op0=OP.mult, op1=OP.add), reads=[b_kk, b_ang], partial=[b_rot])
    kb.op("dve", lambda e: e.scalar_tensor_tensor(out=sinT[:], in0=kk[:], scalar=-TWO_PI_LO, in1=sinT[:], op0=OP.mult, op1=OP.add), reads=[b_kk, b_rot], partial=[b_rot])
    kb.op("dve", lambda e: e.tensor_scalar(out=sinT[:], in0=sinT[:], scalar1=3.141592, scalar2=-3.141592, op0=OP.min, op1=OP.max), reads=[b_rot], partial=[b_rot])
    kb.op("act", lambda e: e.activation(out=cosT[:], in_=sinT[:], func=AF.Abs), reads=[b_rot], partial=[b_rot])
    kb.op("act", lambda e: e.activation(out=cosT[:], in_=cosT[:], func=AF.Sin, scale=-1.0, bias=hpi[:]), reads=[b_rot, b_hpi], partial=[b_rot])
    kb.op("act", lambda e: e.activation(out=sinT[:], in_=sinT[:], func=AF.Sin), reads=[b_rot], partial=[b_rot])
    kb.barrier()
    ar.release(m_r0)
    if debug.get("dump_rot"):
        dump("cosT", cosT[:], b_rot, [128, T], F32)
        dump("sinT", sinT[:], b_rot, [128, T], F32)

    if stop <= 1.5:
        return finish()

    def rotary(ps0, ps1, g, outs, b_outs, tmps, b_tmps, scale_tabs=None):
        gs = slice(g * 512, (g + 1) * 512)
        ta, tb, tc_, td = tmps
        bta, btb, btc, btd = b_tmps
        kb.op("dve", lambda e: e.tensor_tensor(out=ta[:], in0=psf(ps0), in1=cosT[:, gs], op=OP.mult), reads=[PB[ps0], b_rot], writes=[bta])
        kb.op("dve", lambda e: e.tensor_tensor(out=tb[:], in0=psf(ps1), in1=sinT[:, gs], op=OP.mult), reads=[PB[ps1], b_rot], writes=[btb])
        kb.op("dve", lambda e: e.tensor_tensor(out=tc_[:], in0=psf(ps0), in1=sinT[:, gs], op=OP.mult), reads=[PB[ps0], b_rot], writes=[btc])
        kb.op("dve", lambda e: e.tensor_tensor(out=td[:], in0=psf(ps1), in1=cosT[:, gs], op=OP.mult), reads=[PB[ps1], b_rot], writes=[btd])
        if scale_tabs is None:
            kb.op("pool", lambda e: e.tensor_tensor(out=outs[0][:, 0, gs], in0=ta[:], in1=tb[:], op=OP.subtract), reads=[bta, btb], partial=[b_outs[0]])
            kb.op("pool", lambda e: e.tensor_tensor(out=outs[0][:, 1, gs], in0=tc_[:], in1=td[:], op=OP.add), reads=[btc, btd], partial=[b_outs[0]])
        else:
            kb.op("dve", lambda e: e.tensor_tensor(out=ta[:], in0=ta[:], in1=tb[:], op=OP.subtract), reads=[bta, btb], writes=[bta])
            kb.op("dve", lambda e: e.tensor_tensor(out=tc_[:], in0=tc_[:], in1=td[:], op=OP.add), reads=[btc, btd], writes=[btc])
            for k, tab in enumerate(scale_tabs):
                tb4 = tab.unsqueeze(1).to_broadcast([128, 4, 128])
                eng_ = "pool" if k == 0 else "dve"
                kb.op(eng_, lambda e, k=k, tb4=tb4: e.tensor_tensor(out=outs[k][:, 0, gs].rearrange("p (c t) -> p c t", c=4), in0=ta[:].rearrange("p (c t) -> p c t", c=4), in1=tb4, op=OP.mult),
                      reads=[bta, b_tabs], partial=[b_outs[k]])
                kb.op(eng_, lambda e, k=k, tb4=tb4: e.tensor_tensor(out=outs[k][:, 1, gs].rearrange("p (c t) -> p c t", c=4), in0=tc_[:].rearrange("p (c t) -> p c t", c=4), in1=tb4, op=OP.mult),
                      reads=[btc, b_tabs], partial=[b_outs[k]])

    def proj_fm(wt, col0, ncol128, g, banks, b_w):
        gs = slice(g * 512, (g + 1) * 512)
        for j in range(ncol128):
            for dc in range(8):
                kb.op("pe", lambda e, j=j, dc=dc: e.matmul(psf(banks[j]), lhsT=wt[:, dc, col0 + j * 128: col0 + (j + 1) * 128], rhs=hT[:, dc, gs], start=(dc == 0), stop=(dc == 7)),
                      reads=[b_w, b_hT], writes=[PB[banks[j]]] if dc == 0 else (), partial=[PB[banks[j]]] if dc else (), inc=(dc == 7))

    def emit_ag(h):
        kb._emit_waits("pool", [b_cc_src[h]], [b_cc_dst[h]], [])
        nc.gpsimd.collective_compute("AllGather", OP.bypass, replica_groups=[[0, 1, 2, 3], [4, 5, 6, 7]],
                                     ins=[cc_src[h].opt()], outs=[cc_dst[h].opt()]).then_inc(cc_sem, 1)
        kb.cnt["cc"] += 1
        kb.all["cc"] = kb.cnt["cc"]
        b_cc_dst[h].w = {"cc": kb.cnt["cc"]}
        b_cc_src[h].r["cc"] = kb.cnt["cc"]

    m2 = ar.mark()
    wk = [ar.alloc(f"wk{i}", [128, 8, DK], BF16) for i in range(2)]
    wv = [ar.alloc(f"wv{i}", [128, 8, DV], BF16) for i in range(2)]
    b_wk = [Buf(f"wk{i}") for i in range(2)]
    b_wv = [Buf(f"wv{i}") for i in range(2)]
    krT = [ar.alloc(f"krT{i}", [128, 2, T], BF16) for i in range(2)]
    b_krT = [Buf(f"krT{i}") for i in range(2)]
    Vh = [ar.alloc(f"Vh{i}", [128, NT, DV], BF16) for i in range(2)]
    b_Vh = [Buf(f"Vh{i}") for i in range(2)]
    rtm = [[ar.alloc(f"rt{i}{k}", [128, 512], F32) for k in range(4)] for i in range(2)]
    b_rtm = [[Buf(f"rt{i}{k}") for k in range(4)] for i in range(2)]
    kz = [[ar.alloc(f"kz{i}{k}", [128, DK], BF16) for k in range(2)] for i in range(2)]
    b_kz = [[Buf(f"kz{i}{k}") for k in range(2)] for i in range(2)]
    Lsb = ar.alloc("Lsb", [128, 4, DV], F32)
    b_Lsb = Buf("Lsb")

    def load_kv_w(h):
        s = h % 2
        kb.dma("pool", wk[s][:], w_in_v[:, :, C_K + h * DK: C_K + (h + 1) * DK], b_wk[s], reads=[b_in], writes=[b_wk[s]])
        kb.dma("pool", wv[s][:], w_in_v[:, :, C_V + h * DV: C_V + (h + 1) * DV], b_wv[s], reads=[b_in], writes=[b_wv[s]])

    load_kv_w(0)
    for h in range(H):
        s = h % 2
        if h + 1 < H:
            load_kv_w(h + 1)
        for g in range(NG):
            bk = (0, 1) if g % 2 == 0 else (2, 3)
            proj_fm(wk[s], 0, 2, g, bk, b_wk[s])
            rotary(bk[0], bk[1], g, [krT[s]], [b_krT[s]], rtm[g % 2], b_rtm[g % 2])
        if debug.get("p2cut") == 1:
            return finish()
        kb.dma("sp", kT_d[h].rearrange("p (j t) -> p j t", j=2), krT[s][:], b_krT[s], reads=[b_krT[s]], writes=[b_kT_d[h]])
        if debug.get("p2cut") == 2:
            return finish()

        def v_and_t(i, s=s, h=h):
            vb = 4 + (i % 2)
            for dc in range(8):
                kb.op("pe", lambda e, dc=dc: e.matmul(psf(vb), lhsT=hT[:, dc, i * 128:(i + 1) * 128], rhs=wv[s][:, dc, :], start=(dc == 0), stop=(dc == 7)),
                      reads=[b_wv[s], b_hT], writes=[PB[vb]] if dc == 0 else (), partial=[PB[vb]] if dc else (), inc=(dc == 7))
            kb.op("act", lambda e: e.activation(out=Vh[s][:, i, :], in_=psf(vb), func=AF.Copy), reads=[PB[vb]], partial=[b_Vh[s]])
            tb_ = 6 + (i % 2)
            tv = psb(tb_)[:, 0:DK]
            for j in range(2):
                kb.op("pe", lambda e, j=j: e.transpose(out=tv[:, j * 128:(j + 1) * 128], in_=krT[s][:, j, i * 128:(i + 1) * 128], identity=ident[:]),
                      reads=[b_krT[s], b_ident], writes=[PB[tb_]] if j == 0 else (), partial=[PB[tb_]] if j else (), inc=(j == 1))
            kb.op("dve", lambda e: e.tensor_scalar(out=kz[i % 2][0][:], in0=tv, scalar1=zp[:, h, i:i + 1], scalar2=None, op0=OP.mult), reads=[PB[tb_], b_tabs], writes=[b_kz[i % 2][0]])
            kb.op("dve", lambda e: e.tensor_scalar(out=kz[i % 2][1][:], in0=tv, scalar1=zp[:, 4 + h, i:i + 1], scalar2=None, op0=OP.mult), reads=[PB[tb_], b_tabs], writes=[b_kz[i % 2][1]])

        def l_acc(i, s=s):
            for d_ in range(2):
                for c2 in range(2):
                    bank = d_ * 2 + c2
                    kb.op("pe", lambda e, d_=d_, c2=c2, bank=bank: e.matmul(psf(bank), lhsT=kz[i % 2][d_][:, c2 * 128:(c2 + 1) * 128], rhs=Vh[s][:, i, :], start=(i == 0), stop=(i == NT - 1)),
                          reads=[b_kz[i % 2][d_], b_Vh[s]], writes=[PB[bank]] if i == 0 else (), partial=[PB[bank]] if i else (), inc=(d_ == 1 and c2 == 1))

        v_and_t(0)
        if debug.get("p2cut") == 3:
            return finish()
        for i in range(NT):
            if i + 1 < NT:
                v_and_t(i + 1)
            l_acc(i)
        if debug.get("p2cut") == 4:
            return finish()
        for bank in range(4):
            kb.op("act" if bank % 2 else "dve",
                  (lambda e, bank=bank: e.activation(out=Lsb[:, bank, :], in_=psf(bank), func=AF.Copy)) if bank % 2 else
                  (lambda e, bank=bank: e.tensor_copy(out=Lsb[:, bank, :], in_=psf(bank))),
                  reads=[PB[bank]], partial=[b_Lsb])
        for d_ in range(2):
            kb.dma("sp", cc_src[h].rearrange("(d c p) n -> d p c n", d=2, c=2, p=128)[d_], Lsb[:, d_ * 2:(d_ + 1) * 2, :], b_Lsb,
                   reads=[b_Lsb], partial=[b_cc_src[h]])
        if not debug.get("no_cc") and (h < H - 1 or stop <= 2):
            emit_ag(h)
        kb.dma("sp", V_d[h].rearrange("p (i n) -> p i n", i=NT), Vh[s][:], b_Vh[s], reads=[b_Vh[s]], writes=[b_V_d[h]])
    kb.barrier()
    ar.release(m2)
    if debug.get("dump_p1"):
        t = nc.dram_tensor("dbg_ccsrc", [2 * H * 2 * 128, DV], F32, kind="ExternalOutput").ap()
        stg = ar.alloc("stg", [128, 16, DV], F32)
        b_stg = Buf("stg")
        for h in range(H):
            for d_ in range(2):
                kb.dma("sp", stg[:, (d_ * H + h) * 2:(d_ * H + h) * 2 + 2, :], cc_src[h].rearrange("(d c p) n -> d p c n", d=2, c=2, p=128)[d_], b_stg, reads=[b_cc_src[h]], partial=[b_stg])
        kb.dma("sp", t.rearrange("(r p) n -> p r n", p=128), stg[:], b_stg, reads=[b_stg])
        dumps.append("dbg_ccsrc")
        kb.barrier()
    if stop <= 2:
        return finish()

    wq = [ar.alloc(f"wq{i}", [128, 8, DK], BF16) for i in range(2)]
    wg = [ar.alloc(f"wg{i}", [128, 8, DV], BF16) for i in range(2)]
    b_wq = [Buf(f"wq{i}") for i in range(2)]
    b_wg = [Buf(f"wgs{i}") for i in range(2)]
    m4 = ar.mark()
    zT = ar.alloc("zT", [128, 8, T], BF16)
    b_zT = Buf("zT")
    NWC = 3
    wcv = [ar.alloc(f"wcv{i}", [128, 8, 3, 128], BF16) for i in range(NWC)]
    b_wcv = [Buf(f"wcv{i}") for i in range(NWC)]
    ubuf = [ar.alloc(f"u{i}", [128, T + 2], F32) for i in range(2)]
    b_u = [Buf(f"u{i}") for i in range(2)]
    gbb = [ar.alloc(f"gb{i}", [128, T], F32) for i in range(2)]
    b_gb = [Buf(f"gb{i}") for i in range(2)]
    acc = [ar.alloc(f"acc{i}", [128, T], F32) for i in range(2)]
    b_acc = [Buf(f"acc{i}") for i in range(2)]
    gct = [ar.alloc(f"gct{i}", [128, 512], F32) for i in range(2)]
    b_gct = [Buf(f"gct{i}") for i in range(2)]
    hx = [ar.alloc(f"hx{i}", [128, 2], F32) for i in range(2)]
    b_hx = [Buf(f"hx{i}") for i in range(2)]

    def load_conv_w(c):
        s = c % NWC
        for k, col in enumerate((C_XC, C_GC, C_GB)):
            kb.dma("pool", wcv[s][:, :, k, :], w_in_v[:, :, col + c * 128: col + (c + 1) * 128], b_wcv[s], reads=[b_in],
                   writes=[b_wcv[s]] if k == 0 else (), partial=[b_wcv[s]] if k else ())

    cc_v = [cc_dst[h].rearrange("(r d c p) n -> r d p c n", r=NSEG, d=2, c=2, p=128) for h in range(H)]
    sacc = [ar.alloc(f"sacc{i}", [128, 2, DV], F32) for i in range(2)]
    slin = [ar.alloc(f"slin{i}", [128, 2, DV], F32) for i in range(2)]
    b_sacc = [Buf(f"sacc{i}") for i in range(2)]
    b_slin = [Buf(f"slin{i}") for i in range(2)]
    sin_k = [0]

    def sin_pre(h, d_):
        a_ = (h * 2 + d_) % 2
        accv = sacc[a_][:]
        ranks = (0, 1, 2) if d_ == 0 else (1, 2, 3)
        for ri, r in enumerate(ranks):
            lb = sin_k[0] % 2
            sin_k[0] += 1
            linv = slin[lb][:]
            kb.dma("sp", linv, cc_v[h][r, d_], b_slin[lb], reads=[b_cc_dst[h]], writes=[b_slin[lb]])
            wsc = wseg[:, d_ * 4 + h, r:r + 1]
            if ri == 0:
                kb.op("dve", lambda e, wsc=wsc, accv=accv, linv=linv: e.tensor_scalar(out=accv, in0=linv, scalar1=wsc, scalar2=None, op0=OP.mult),
                      reads=[b_slin[lb], b_tabs], writes=[b_sacc[a_]])
            else:
                kb.op("dve", lambda e, wsc=wsc, accv=accv, linv=linv: e.scalar_tensor_tensor(out=accv, in0=linv, scalar=wsc, in1=accv, op0=OP.mult, op1=OP.add),
                      reads=[b_slin[lb], b_tabs, b_sacc[a_]], writes=[b_sacc[a_]])
        kb.dma("sp", sin_d[h * 2 + d_].rearrange("p (c n) -> p c n", c=2), accv, b_sacc[a_], reads=[b_sacc[a_]], writes=[b_sin_d[h * 2 + d_]])

    load_conv_w(0)
    load_conv_w(1)
    for c in range(8):
        s = c % 2
        ws = c % NWC
        if c + 2 < 8:
            load_conv_w(c + 2)
        if c == 1 and not debug.get("no_cc"):
            emit_ag(H - 1)
        for k in range(2):
            for dc in range(8):
                kb.op("pe", lambda e, k=k, dc=dc: e.matmul(ps_all[:, 6, 2 * k:2 * k + 2], lhsT=wcv[ws][:, dc, k, :], rhs=hTh[:, dc, :], start=(dc == 0), stop=(dc == 7)),
                      reads=[b_wcv[ws], b_hTh], writes=[PB[6]] if (dc == 0 and k == 0) else (), partial=[PB[6]] if (dc or k) else (), inc=(dc == 7 and k == 1))
        kb.op("act", lambda e: e.activation(out=hx[s][:], in_=ps_all[:, 6, 2:4], func=AF.Copy), reads=[PB[6]], writes=[b_hx[s]])
        kb.op("dve", lambda e: e.tensor_tensor(out=ubuf[s][:, 0:T + 2:T + 1], in0=ps_all[:, 6, 0:2], in1=hx[s][:], op=OP.mult), reads=[PB[6], b_hx[s]], writes=[b_u[s]])
        for g in range(NG):
            gs = slice(g * 512, (g + 1) * 512)
            bk = (0, 1, 2) if g % 2 == 0 else (3, 4, 5)
            for k in range(3):
                for dc in range(8):
                    kb.op("pe", lambda e, k=k, dc=dc: e.matmul(psf(bk[k]), lhsT=wcv[ws][:, dc, k, :], rhs=hT[:, dc, gs], start=(dc == 0), stop=(dc == 7)),
                          reads=[b_wcv[ws], b_hT], writes=[PB[bk[k]]] if dc == 0 else (), partial=[PB[bk[k]]] if dc else (), inc=(dc == 7))
            kb.op("act", lambda e: e.activation(out=gct[g % 2][:], in_=psf(bk[1]), func=AF.Copy), reads=[PB[bk[1]]], writes=[b_gct[g % 2]])
            kb.op("act", lambda e: e.activation(out=gbb[s][:, gs], in_=psf(bk[2]), func=AF.Copy), reads=[PB[bk[2]]], partial=[b_gb[s]])
            kb.op("dve", lambda e: e.tensor_tensor(out=ubuf[s][:, 1 + g * 512: 1 + (g + 1) * 512], in0=psf(bk[0]), in1=gct[g % 2][:], op=OP.mult),
                  reads=[PB[bk[0]], b_gct[g % 2]], partial=[b_u[s]])
        w0 = wconvT[:, c * 3 + 0: c * 3 + 1]
        w1 = wconvT[:, c * 3 + 1: c * 3 + 2]
        w2 = wconvT[:, c * 3 + 2: c * 3 + 3]
        kb.op("dve", lambda e: e.tensor_scalar(out=acc[s][:], in0=ubuf[s][:, 0:T], scalar1=w0, scalar2=None, op0=OP.mult), reads=[b_u[s], cbuf], writes=[b_acc[s]])
        kb.op("dve", lambda e: e.scalar_tensor_tensor(out=acc[s][:], in0=ubuf[s][:, 1:T + 1], scalar=w1, in1=acc[s][:], op0=OP.mult, op1=OP.add), reads=[b_u[s], cbuf, b_acc[s]], writes=[b_acc[s]])
        kb.op("dve", lambda e: e.scalar_tensor_tensor(out=acc[s][:], in0=ubuf[s][:, 2:T + 2], scalar=w2, in1=acc[s][:], op0=OP.mult, op1=OP.add), reads=[b_u[s], cbuf, b_acc[s]], writes=[b_acc[s]])
        kb.op("dve", lambda e: e.tensor_tensor(out=zT[:, c, :], in0=acc[s][:], in1=gbb[s][:], op=OP.mult), reads=[b_acc[s], b_gb[s]], partial=[b_zT])
        sin_pre(c // 2, c % 2)
    kb.dma("sp", z_d.rearrange("p (c t) -> p c t", c=8), zT[:], b_zT, reads=[b_zT], writes=[b_z_d])
    def load_qg_w(h):
        s = h % 2
        kb.dma("pool", wq[s][:], w_in_v[:, :, C_Q + h * DK: C_Q + (h + 1) * DK], b_wq[s], reads=[b_in], writes=[b_wq[s]])
        kb.dma("pool", wg[s][:], w_in_v[:, :, C_GSW + h * DV: C_GSW + (h + 1) * DV], b_wg[s], reads=[b_in], writes=[b_wg[s]])

    if stop > 4:
        load_qg_w(0)
    kb.barrier()
    ar.release(m4)
    if debug.get("dump_z"):
        stg = ar.alloc("stgz", [128, 8, T], BF16)
        b_stg = Buf("stgz")
        kb.dma("sp", stg[:], z_d.rearrange("p (c t) -> p c t", c=8), b_stg, reads=[b_z_d], writes=[b_stg])
        dump("zT", stg[:], b_stg, [128, 8, T], BF16)
        kb.barrier()
    if stop <= 4:
        return finish()

    m5 = ar.mark()
    gretb = [ar.alloc(f"gretb{i}", [128, DV], F32) for i in range(2)]
    b_gretb = [Buf(f"gretb{i}") for i in range(2)]
    krT2 = ar.alloc("krT2", [128, 2, T], BF16)
    V2 = ar.alloc("V2", [128, NT, DV], BF16)
    b_krT2, b_V2 = Buf("krT2"), Buf("V2")
    QfT = ar.alloc("QfT", [128, 2, T], BF16)
    QbT = ar.alloc("QbT", [128, 2, T], BF16)
    b_QfT, b_QbT = Buf("QfT"), Buf("QbT")
    SbAll = ar.alloc("SbAll", [128, NT, 2, DV], BF16)
    b_SbAll = [Buf(f"SbAll{c}") for c in range(NT)]
    Sst = ar.alloc("Sst", [128, 2, DV], F32)
    b_Sst = Buf("Sst")
    Sfb = [ar.alloc(f"Sfb{i}", [128, 2, DV], BF16) for i in range(2)]
    b_Sfb = [Buf(f"Sfb{i}") for i in range(2)]
    gTh = [ar.alloc(f"gTh{i}", [128, 4, 512], BF16) for i in range(2)]
    b_gTh = [Buf(f"gTh{i}") for i in range(2)]
    sgAll = ar.alloc("sgAll", [128, NT, DV], BF16)
    b_sgAll = [Buf(f"sgAll{c}") for c in range(NT)]
    rtL = ar.alloc("rtL", [128, 4, 512], F32)
    rt2 = [rtL[:, k, :] for k in range(4)]
    b_rt2 = [Buf(f"r2{k}") for k in range(4)]
    Lin = [rtL[:, 0:2, :], None]
    b_mv2 = [Buf(f"mv2{i}") for i in range(2)]
    b_mv3 = [Buf(f"mv3{i}") for i in range(2)]
    kz2 = [ar.alloc(f"kz2{i}", [128, DK], BF16) for i in range(2)]
    b_kz2 = [Buf(f"kz2{i}") for i in range(2)]
    PTs = [ar.alloc(f"PT{i}", [128, 128], BF16) for i in range(2)]
    b_PTs = [Buf(f"PT{i}") for i in range(2)]
    st6 = [ar.alloc(f"st6{i}", [128, 6], F32) for i in range(2)]
    mv = [ar.alloc(f"mv{i}", [128, 4], F32) for i in range(2)]
    b_st6 = [Buf(f"st6{i}") for i in range(2)]
    b_mv = [Buf(f"mv{i}") for i in range(2)]
    ynL = ar.alloc("ynL", [128, 2, DV], F32)
    yn = [ynL[:, i, :] for i in range(2)]
    gat = [ar.alloc(f"gat{i}", [128, DV], BF16) for i in range(2)]
    b_yn = [Buf(f"yn{i}") for i in range(2)]
    b_sg = [Buf(f"sg{i}") for i in range(2)]
    b_gat = [Buf(f"gat{i}") for i in range(2)]

    def incoming_state(h, d_):
        kb.dma("sp", Sst[:], sin_d[h * 2 + d_].rearrange("p (c n) -> p c n", c=2), b_Sst, reads=[b_sin_d[h * 2 + d_]], writes=[b_Sst])

    def kt_scaled(c, zc, slot):
        tv = psb(3)[:, 0:DK]
        for j in range(2):
            kb.op("pe", lambda e, j=j: e.transpose(out=tv[:, j * 128:(j + 1) * 128], in_=krT2[:, j, c * 128:(c + 1) * 128], identity=ident[:]),
                  reads=[b_krT2, b_ident], writes=[PBx["kt"]] if j == 0 else (), partial=[PBx["kt"]] if j else (), inc=(j == 1))
        kb.op("dve", lambda e: e.tensor_scalar(out=kz2[slot][:], in0=tv, scalar1=zc, scalar2=None, op0=OP.mult), reads=[PBx["kt"], b_tabs], writes=[b_kz2[slot]])

    PBx = {"kt": PB[3], "pt": PB[2], "gt": PB[4]}

    def state_update(c, slot, ubank, gcol):
        for c2 in range(2):
            kb.op("pe", lambda e, c2=c2: e.matmul(psf(ubank + c2), lhsT=kz2[slot][:, c2 * 128:(c2 + 1) * 128], rhs=V2[:, c, :], start=True, stop=True),
                  reads=[b_kz2[slot], b_V2], writes=[PB[ubank + c2]], inc=(c2 == 1))
        kb.op("dve", lambda e: e.scalar_tensor_tensor(out=Sst[:], in0=Sst[:], scalar=gcol, in1=ps_all[:, ubank:ubank + 2, :], op0=OP.mult, op1=OP.add),
              reads=[b_Sst, PB[ubank], PB[ubank + 1], b_tabs], writes=[b_Sst])

    for h in range(H):
        s = h % 2
        if h + 1 < H:
            load_qg_w(h + 1)
        kb.dma("sp", gretb[s][:], gret_in[0:1, h * DV:(h + 1) * DV].partition_broadcast(128), b_gretb[s], reads=[b_in], writes=[b_gretb[s]])
        kb.dma("sp", krT2[:], kT_d[h].rearrange("p (j t) -> p j t", j=2), b_krT2, reads=[b_kT_d[h]], writes=[b_krT2])
        kb.dma("sp", V2[:], V_d[h].rearrange("p (i n) -> p i n", i=NT), b_V2, reads=[b_V_d[h]], writes=[b_V2])
        for g in range(NG):
            bk = (0, 1) if g % 2 == 0 else (2, 3)
            proj_fm(wq[s], 0, 2, g, bk, b_wq[s])
            rotary(bk[0], bk[1], g, [QfT, QbT], [b_QfT, b_QbT], rt2, b_rt2, scale_tabs=[xif[:, h, :], xib[:, h, :]])
        incoming_state(h, 1)
        for c in range(NT - 1, -1, -1):
            cs = slice(c * 128, (c + 1) * 128)
            kb.op("act", lambda e, c=c: e.activation(out=SbAll[:, c, :, :], in_=Sst[:], func=AF.Copy), reads=[b_Sst], writes=[b_SbAll[c]])
            if c > 0:
                kt_scaled(c, zcol[:, 4 + h:5 + h], c % 2)
            gb_ = 6 + (c % 2)
            for dc in range(8):
                kb.op("pe", lambda e, dc=dc: e.matmul(psf(gb_), lhsT=hT[:, dc, cs], rhs=wg[s][:, dc, :], start=(dc == 0), stop=(dc == 7)),
                      reads=[b_wg[s], b_hT], writes=[PB[gb_]] if dc == 0 else (), partial=[PB[gb_]] if dc else (), inc=(dc == 7))
            kb.op("act", lambda e, c=c: e.activation(out=yn[c % 2][:], in_=psf(gb_), func=AF.Silu), reads=[PB[gb_]], writes=[b_yn[c % 2]])
            kb.op("pool", lambda e, c=c: e.tensor_tensor(out=sgAll[:, c, :], in0=yn[c % 2][:], in1=gretb[s][:], op=OP.mult), reads=[b_yn[c % 2], b_gretb[s]], writes=[b_sgAll[c]])
            if c > 0:
                state_update(c, c % 2, 0, g128[:, 4 + h:5 + h])
        incoming_state(h, 0)
        kb.op("act", lambda e: e.activation(out=Sfb[0][:], in_=Sst[:], func=AF.Copy), reads=[b_Sst], writes=[b_Sfb[0]])

        def gated_T(c, h=h):
            p2 = c % 2
            gq, ci = divmod(c, 4)
            slot = gq % 2
            for half in range(2):
                tv = psb(4)[:, half * 256:(half + 1) * 256].rearrange("p (a t) -> p a t", a=2)
                for a in range(2):
                    ec = half * 2 + a
                    kb.op("pe", lambda e, a=a, ec=ec, tv=tv: e.transpose(out=tv[:, a, :], in_=gat[p2][:, ec * 128:(ec + 1) * 128], identity=ident[:]),
                          reads=[b_gat[p2], b_ident], writes=[PBx["gt"]] if (half == 0 and a == 0) else (), partial=[PBx["gt"]] if (half or a) else (), inc=(a == 1 and half == 1))
            tv4 = psb(4)[:, 0:512].rearrange("p (a t) -> p a t", a=4)
            kb.op("act", lambda e: e.activation(out=gTh[slot][:, :, ci * 128:(ci + 1) * 128], in_=tv4, func=AF.Copy), reads=[PBx["gt"]], partial=[b_gTh[slot]])
            if ci == 3:
                kb.dma("sp", gT_d[h].rearrange("p (a t) -> p a t", a=4)[:, :, gq * 512:(gq + 1) * 512], gTh[slot][:], b_gTh[slot], reads=[b_gTh[slot]], partial=[b_gT_d[h]])

        def gn_tail(c, h=h):
            p2 = c % 2
            ob = 5 + p2
            kb.op("dve", lambda e: e.reciprocal(out=mv[p2][:, 3:4], in_=mv[p2][:, 2:3]), reads=[b_mv2[p2]], writes=[b_mv3[p2]])
            kb.op("dve", lambda e: e.tensor_scalar(out=mv[p2][:, 2:3], in0=mv[p2][:, 0:1], scalar1=mv[p2][:, 3:4], scalar2=-1.0, op0=OP.mult, op1=OP.mult), reads=[b_mv[p2], b_mv3[p2]], writes=[b_mv2[p2]])
            kb.op("act", lambda e: e.activation(out=yn[p2][:], in_=psf(ob), func=AF.Identity, scale=mv[p2][:, 3:4], bias=mv[p2][:, 2:3]), reads=[PB[ob], b_mv2[p2], b_mv3[p2]], writes=[b_yn[p2]])
            kb.op("pool", lambda e: e.tensor_tensor(out=gat[p2][:], in0=yn[p2][:], in1=sgAll[:, c, :], op=OP.mult),
                  reads=[b_yn[p2], b_sgAll[c]], writes=[b_gat[p2]])

        for c in range(NT):
            cs = slice(c * 128, (c + 1) * 128)
            p2 = c % 2
            ob = 5 + p2
            for j2 in range(2):
                kb.op("pe", lambda e, j2=j2: e.matmul(ps_all[:, 2, 0:128], lhsT=krT2[:, j2, cs], rhs=QfT[:, j2, cs], start=(j2 == 0), stop=(j2 == 1)),
                      reads=[b_krT2, b_QfT], writes=[PBx["pt"]] if j2 == 0 else (), partial=[PBx["pt"]] if j2 else (), inc=(j2 == 1))
            kb.op("dve", lambda e: e.tensor_tensor(out=PTs[p2][:], in0=ps_all[:, 2, 0:128], in1=maskT[:, h, :], op=OP.mult), reads=[PBx["pt"], b_tabs], writes=[b_PTs[p2]])
            if c < NT - 1:
                if c == 0:
                    kt_scaled(0, zcol[:, h:h + 1], 0)
                state_update(c, p2, 0, g128[:, h:h + 1])
                if c + 1 < NT - 1:
                    kt_scaled(c + 1, zcol[:, h:h + 1], 1 - p2)
                kb.op("act", lambda e: e.activation(out=Sfb[1 - p2][:], in_=Sst[:], func=AF.Copy), reads=[b_Sst], writes=[b_Sfb[1 - p2]])
            kb.op("pe", lambda e: e.matmul(psf(ob), lhsT=PTs[p2][:], rhs=V2[:, c, :], start=True, stop=False), reads=[b_PTs[p2], b_V2], writes=[PB[ob]], inc=False)
            for j2 in range(2):
                kb.op("pe", lambda e, j2=j2: e.matmul(psf(ob), lhsT=QbT[:, j2, cs], rhs=SbAll[:, c, j2, :], start=False, stop=False), reads=[b_QbT, b_SbAll[c]], partial=[PB[ob]], inc=False)
            for j2 in range(2):
                kb.op("pe", lambda e, j2=j2: e.matmul(psf(ob), lhsT=QfT[:, j2, cs], rhs=Sfb[p2][:, j2, :], start=False, stop=(j2 == 1)), reads=[b_QfT, b_Sfb[p2]], partial=[PB[ob]], inc=(j2 == 1))
            if c > 0:
                gn_tail(c - 1)
            kb.op("dve", lambda e: e.bn_stats(out=st6[p2][:], in_=psf(ob)), reads=[PB[ob]], writes=[b_st6[p2]])
            kb.op("dve", lambda e: e.bn_aggr(out=mv[p2][:, 0:2], in_=st6[p2][:]), reads=[b_st6[p2]], writes=[b_mv[p2]])
            kb.op("act", lambda e: e.activation(out=mv[p2][:, 2:3], in_=mv[p2][:, 1:2], func=AF.Sqrt, bias=epsc[:]), reads=[b_mv[p2], b_eps], writes=[b_mv2[p2]])
            if c > 1:
                gated_T(c - 2)
        gn_tail(NT - 1)
        gated_T(NT - 2)
        gated_T(NT - 1)
    kb.barrier()
    ar.release(m5)
    ar.release(m_ret)
    if debug.get("dump_g"):
        stg = ar.alloc("stgg", [128, 16, T], BF16)
        b_stg = Buf("stgg")
        for h in range(H):
            kb.dma("sp", stg[:, h * 4:(h + 1) * 4, :], gT_d[h].rearrange("p (a t) -> p a t", a=4), b_stg, reads=[b_gT_d[h]], partial=[b_stg])
        dump("gT", stg[:], b_stg, [128, 16, T], BF16)
        kb.barrier()
    if stop <= 5:
        return finish()

    mT = ar.alloc("mT", [128, 8, T], BF16)
    m6 = ar.mark()
    zT6 = ar.alloc("zT6", [128, 8, T], BF16)
    gT6 = ar.alloc("gT6", [128, 16, T], BF16)
    b_mT = Buf("mT")
    b_zT6 = [Buf(f"zT6_{g}") for g in range(NG)]
    b_gT6 = [Buf(f"gT6_{g}") for g in range(NG)]
    for g in range(NG):
        gs = slice(g * 512, (g + 1) * 512)
        kb.dma("sp", zT6[:, :, gs], z_d.rearrange("p (c t) -> p c t", c=8)[:, :, gs], b_zT6[g], reads=[b_z_d], writes=[b_zT6[g]])
        for h in range(H):
            kb.dma("sp", gT6[:, h * 4:(h + 1) * 4, gs], gT_d[h].rearrange("p (a t) -> p a t", a=4)[:, :, gs], b_gT6[g], reads=[b_gT_d[h]],
                   writes=[b_gT6[g]] if h == 0 else (), partial=[b_gT6[g]] if h else ())
    wm = [ar.alloc(f"wm{i}", [128, 40, 128], BF16) for i in range(2)]
    b_wm = [Buf(f"wm{i}") for i in range(2)]
    sa = [ar.alloc(f"sa{i}", [128, 512], F32) for i in range(2)]
    sr = [ar.alloc(f"sr{i}", [128, 512], F32) for i in range(2)]
    t1 = [ar.alloc(f"t1{i}", [128, 512], F32) for i in range(2)]
    b_sa = [Buf(f"sa{i}") for i in range(2)]
    b_sr = [Buf(f"sr{i}") for i in range(2)]
    b_t1 = [Buf(f"t1{i}") for i in range(2)]
    w_a_v = w_a_out.rearrange("(c p) n -> p c n", p=128)
    w_r_v = w_r_out.rearrange("(c p) n -> p c n", p=128)

    def load_m_w(dcn):
        s = dcn % 2
        cs_ = slice(dcn * 128, (dcn + 1) * 128)
        kb.dma("pool", wm[s][:, 0:8, :], w_a_v[:, :, cs_], b_wm[s], reads=[b_in], writes=[b_wm[s]])
        kb.dma("pool", wm[s][:, 8:24, :], w_r_v[:, :, cs_], b_wm[s], reads=[b_in], partial=[b_wm[s]])
        kb.dma("pool", wm[s][:, 24:32, :], w_in_v[:, :, C_GA + dcn * 128: C_GA + (dcn + 1) * 128], b_wm[s], reads=[b_in], partial=[b_wm[s]])
        kb.dma("pool", wm[s][:, 32:40, :], w_in_v[:, :, C_GR + dcn * 128: C_GR + (dcn + 1) * 128], b_wm[s], reads=[b_in], partial=[b_wm[s]])

    def mm_group(bank, items, reads):
        n = len(items)
        for k, (l, r) in enumerate(items):
            kb.op("pe", lambda e, l=l, r=r, k=k: e.matmul(psf(bank), lhsT=l, rhs=r, start=(k == 0), stop=(k == n - 1)),
                  reads=reads, writes=[PB[bank]] if k == 0 else (), partial=[PB[bank]] if k else (), inc=(k == n - 1))

    load_m_w(0)
    it = 0
    for dcn in range(8):
        s = dcn % 2
        if dcn + 1 < 8:
            load_m_w(dcn + 1)
        for g in range(NG):
            gs = slice(g * 512, (g + 1) * 512)
            q = it % 2
            it += 1
            bk = (0, 1, 2, 3) if q == 0 else (4, 5, 6, 7)
            mm_group(bk[2], [(wm[s][:, 24 + dc, :], hT[:, dc, gs]) for dc in range(8)], [b_wm[s], b_hT])
            mm_group(bk[3], [(wm[s][:, 32 + dc, :], hT[:, dc, gs]) for dc in range(8)], [b_wm[s], b_hT])
            mm_group(bk[0], [(wm[s][:, c, :], zT6[:, c, gs]) for c in range(8)], [b_wm[s], b_zT6[g]])
            mm_group(bk[1], [(wm[s][:, 8 + c, :], gT6[:, c, gs]) for c in range(16)], [b_wm[s], b_gT6[g]])
            kb.op("act", lambda e: e.activation(out=sa[q][:], in_=psf(bk[2]), func=AF.Sigmoid), reads=[PB[bk[2]]], writes=[b_sa[q]])
            kb.op("act", lambda e: e.activation(out=sr[q][:], in_=psf(bk[3]), func=AF.Sigmoid), reads=[PB[bk[3]]], writes=[b_sr[q]])
            kb.op("dve", lambda e: e.tensor_tensor(out=t1[q][:], in0=psf(bk[0]), in1=sa[q][:], op=OP.mult), reads=[PB[bk[0]], b_sa[q]], writes=[b_t1[q]])
            kb.op("dve", lambda e: e.tensor_tensor(out=sr[q][:], in0=psf(bk[1]), in1=sr[q][:], op=OP.mult), reads=[PB[bk[1]], b_sr[q]], writes=[b_sr[q]])
            kb.op("dve", lambda e: e.tensor_tensor(out=mT[:, dcn, gs], in0=t1[q][:], in1=sr[q][:], op=OP.add), reads=[b_t1[q], b_sr[q]], partial=[b_mT])
    kb.barrier()
    if debug.get("dump_m"):
        dump("mT", mT[:], b_mT, [128, 8, T], BF16)
        kb.barrier()
    if stop <= 6:
        return finish()


    ar.release(m6)
    m7 = ar.mark()
    wo = ar.alloc("wo", [128, 8, D], BF16)
    b_wo = Buf("wo")
    w_o_v = w_o.rearrange("(c p) n -> p c n", p=128)
    kb.dma("pool", wo[:, :, 0:512], w_o_v[:, :, 0:512], b_wo, reads=[b_in], writes=[b_wo])
    kb.dma("pool", wo[:, :, 512:1024], w_o_v[:, :, 512:1024], b_wo, reads=[b_in], partial=[b_wo])
    xr = [ar.alloc(f"xr{i}", [128, D], F32) for i in range(2)]
    x1t = [ar.alloc(f"x1t{i}", [128, D], F32) for i in range(2)]
    b_xr = [Buf(f"xr{i}") for i in range(2)]
    b_x1t = [Buf(f"x1t{i}") for i in range(2)]
    junk7 = ar.alloc("junk7", [128, D], BF16)
    b_junk7 = Buf("junk7")
    xs7 = [ar.alloc(f"xs7{i}", [128, D], BF16) for i in range(2)]
    b_xs7 = [Buf(f"xs7{i}") for i in range(2)]
    st7 = [ar.alloc(f"st7{i}", [128, 4], F32) for i in range(2)]
    b_st7 = [Buf(f"st7{i}") for i in range(2)]
    kb.dma("sp", xr[0][:], x_seg[0:128, :], b_xr[0], reads=[b_in], writes=[b_xr[0]])
    pend7 = []
    for i in range(NT):
        q = i % 2
        if i + 1 < NT:
            kb.dma("sp", xr[1 - q][:], x_seg[(i + 1) * 128:(i + 2) * 128, :], b_xr[1 - q], reads=[b_in], writes=[b_xr[1 - q]])
        b0 = q * 2
        for half in range(2):
            mm_group(b0 + half, [(mT[:, dc, i * 128:(i + 1) * 128], wo[:, dc, half * 512:(half + 1) * 512]) for dc in range(8)], [b_mT, b_wo])
        kb.op("dve", lambda e: e.tensor_tensor(out=x1t[q][:].rearrange("p (a n) -> p a n", a=2), in0=ps_all[:, b0:b0 + 2, :], in1=xr[q][:].rearrange("p (a n) -> p a n", a=2), op=OP.add),
              reads=[PB[b0], PB[b0 + 1], b_xr[q]], writes=[b_x1t[q]])
        kb.dma("sp", x1_d[i * 128:(i + 1) * 128, :], x1t[q][:], b_x1t[q], reads=[b_x1t[q]], writes=[b_x1_d[i]])
        norm_to_T(x1t[q], b_x1t[q], 128, None, b_hT, gffnT, junk7, b_junk7, xs7[q], b_xs7[q], st7[q], b_st7[q], 4 + q, defer=True)
        pend7.append((lambda i=i: hT[:, :, i * 128:(i + 1) * 128], q))
        if len(pend7) > 1:
            fn_, q_ = pend7.pop(0)
            norm_B(128, fn_, b_hT, gffnT, xs7[q_], b_xs7[q_], 4 + q_)
    for fn_, q_ in pend7:
        norm_B(128, fn_, b_hT, gffnT, xs7[q_], b_xs7[q_], 4 + q_)
    kb.barrier()
    ar.release(m7)
    ar.release(m6)
    if debug.get("dump_h2"):
        dump("h2T", hT[:], b_hT, [128, 8, T], BF16)
        kb.barrier()
    if stop <= 7:
        return finish()

    ar.top = m6 - 8 * T * 2
    aT = ar.alloc("aT", [128, NFF, T], BF16)
    b_aT = Buf("aT")
    wd = ar.alloc("wd", [128, NFF, D], BF16)
    b_wd = Buf("wd")
    m8 = ar.mark()
    wgu = [ar.alloc(f"wgu{i}", [128, 8, 2, 128], BF16) for i in range(2)]
    b_wgu = [Buf(f"wgu{i}") for i in range(2)]
    sgf = [ar.alloc(f"sgf{i}", [128, 512], F32) for i in range(2)]
    b_sgf = [Buf(f"sgf{i}") for i in range(2)]
    w_g_v = w_gate.rearrange("(c p) n -> p c n", p=128)
    w_u_v = w_up.rearrange("(c p) n -> p c n", p=128)
    w_d_v = w_down.rearrange("(c p) n -> p c n", p=128)

    def load_gu(f):
        s = f % 2
        kb.dma("pool", wgu[s][:, :, 0, :], w_g_v[:, :, f * 128:(f + 1) * 128], b_wgu[s], reads=[b_in], writes=[b_wgu[s]])
        kb.dma("pool", wgu[s][:, :, 1, :], w_u_v[:, :, f * 128:(f + 1) * 128], b_wgu[s], reads=[b_in], partial=[b_wgu[s]])

    load_gu(0)
    it = 0
    for f in range(NFF):
        s = f % 2
        if debug.get("ffskip") and f >= 8:
            continue
        if f + 1 < NFF:
            load_gu(f + 1)
        if f in (1, 2, 3, 4):
            qd = f - 1
            kb.dma("pool", wd[:, :, qd * 256:(qd + 1) * 256], w_d_v[:, :, qd * 256:(qd + 1) * 256], b_wd, reads=[b_in],
                   writes=[b_wd] if qd == 0 else (), partial=[b_wd] if qd else ())
        for g in range(NG):
            gs = slice(g * 512, (g + 1) * 512)
            q = it % 2
            it += 1
            bk = (0, 1) if q == 0 else (2, 3)
            mm_group(bk[0], [(wgu[s][:, dc, 0, :], hT[:, dc, gs]) for dc in range(8)], [b_wgu[s], b_hT])
            mm_group(bk[1], [(wgu[s][:, dc, 1, :], hT[:, dc, gs]) for dc in range(8)], [b_wgu[s], b_hT])
            kb.op("act", lambda e: e.activation(out=sgf[q][:], in_=psf(bk[0]), func=AF.Silu), reads=[PB[bk[0]]], writes=[b_sgf[q]])
            kb.op("dve", lambda e: e.tensor_tensor(out=aT[:, f, gs], in0=psf(bk[1]), in1=sgf[q][:], op=OP.mult), reads=[PB[bk[1]], b_sgf[q]], partial=[b_aT])
    kb.barrier()
    if debug.get("dump_a"):
        dump("aT", aT[:], b_aT, [128, NFF, T], BF16)
        kb.barrier()
    if stop <= 8:
        return finish()

    ar.release(m8)
    gfinb = ar.alloc("gfinb", [128, D], F32)
    b_gfinb = Buf("gfinb")
    kb.dma("sp", gfinb[:], gfin_in[0:1, :].partition_broadcast(128), b_gfinb, reads=[b_in], writes=[b_gfinb])
    x1r = [ar.alloc(f"x1r{i}", [128, D], F32) for i in range(2)]
    x2 = x1r
    ot = [ar.alloc(f"ot{i}", [128, D], F32) for i in range(2)]
    b_x1r = [Buf(f"x1r{i}") for i in range(2)]
    b_x2 = b_x1r
    b_ot = [Buf(f"ot{i}") for i in range(2)]
    junk9 = ar.alloc("junk9", [128, D], BF16)
    b_junk9 = Buf("junk9")
    st9 = [ar.alloc(f"st9{i}", [128, 4], F32) for i in range(2)]
    b_st9 = [Buf(f"st9{i}") for i in range(2)]
    kb.dma("sp", x1r[0][:], x1_d[0:128, :], b_x1r[0], reads=[b_x1_d[0]], writes=[b_x1r[0]])
    for i in range(NT):
        q = i % 2
        if i + 1 < NT:
            kb.dma("sp", x1r[1 - q][:], x1_d[(i + 1) * 128:(i + 2) * 128, :], b_x1r[1 - q], reads=[b_x1_d[i + 1]], writes=[b_x1r[1 - q]])
        b0 = q * 2
        for half in range(2):
            mm_group(b0 + half, [(aT[:, f, i * 128:(i + 1) * 128], wd[:, f, half * 512:(half + 1) * 512]) for f in range(NFF)], [b_aT, b_wd])
        kb.op("dve", lambda e: e.tensor_tensor(out=x2[q][:].rearrange("p (a n) -> p a n", a=2), in0=ps_all[:, b0:b0 + 2, :], in1=x1r[q][:].rearrange("p (a n) -> p a n", a=2), op=OP.add),
              reads=[PB[b0], PB[b0 + 1]], writes=[b_x2[q]])
        kb.op("act", lambda e: e.activation(out=junk9[:], in_=x2[q][:], func=AF.Square, accum_out=st9[q][:, 0:1]), reads=[b_x2[q]], writes=[b_junk9, b_st9[q]], multi=True)
        kb.op("act", lambda e: e.activation(out=st9[q][:, 1:2], in_=st9[q][:, 0:1], func=AF.Sqrt, scale=1.0 / D, bias=epsc[:]), reads=[b_st9[q], b_eps], writes=[b_st9[q]])
        kb.op("dve", lambda e: e.reciprocal(out=st9[q][:, 2:3], in_=st9[q][:, 1:2]), reads=[b_st9[q]], writes=[b_st9[q]])
        kb.op("dve", lambda e: e.scalar_tensor_tensor(out=ot[q][:], in0=x2[q][:], scalar=st9[q][:, 2:3], in1=gfinb[:], op0=OP.mult, op1=OP.mult),
              reads=[b_x2[q], b_st9[q], b_gfinb], writes=[b_ot[q]])
        kb.dma("sp", y_out[i * 128:(i + 1) * 128, :], ot[q][:], b_ot[q], reads=[b_ot[q]])
    return finish()


_FREQ = (10000.0 ** (-(np.arange(0, DK, 2, dtype=np.float32) / np.float32(DK)))).astype(np.float32).reshape(128, 1)


def make_in_maps(x, g_mix, w_in, w_conv, dec_f, dec_b, g_ret, w_a_out, w_r_out, w_o,
                 g_ffn, w_ff_gate, w_ff_up, w_ff_down, g_final):
    f = np.float32
    c_ = np.ascontiguousarray
    shared = {
        "freq": _FREQ,
        "dec": c_(np.concatenate([np.asarray(dec_f, f).reshape(1, 4), np.asarray(dec_b, f).reshape(1, 4)], axis=1)),
        "g_mixT": c_(np.asarray(g_mix, f).reshape(8, 128).T),
        "g_ffnT": c_(np.asarray(g_ffn, f).reshape(8, 128).T),
        "w_convT": c_(np.asarray(w_conv, f).reshape(3, 8, 128).transpose(2, 1, 0).reshape(128, 24)),
        "g_ret": c_(np.asarray(g_ret, f).reshape(1, H * DV)),
        "g_final": c_(np.asarray(g_final, f).reshape(1, D)),
        "w_in": c_(np.asarray(w_in, f).reshape(D, NIN)),
        "w_a_out": c_(np.asarray(w_a_out, f).reshape(D, D)),
        "w_r_out": c_(np.asarray(w_r_out, f).reshape(H * DV, D)),
        "w_o": c_(np.asarray(w_o, f).reshape(D, D)),
        "w_ff_gate": c_(np.asarray(w_ff_gate, f).reshape(D, DFF)),
        "w_ff_up": c_(np.asarray(w_ff_up, f).reshape(D, DFF)),
        "w_ff_down": c_(np.asarray(w_ff_down, f).reshape(DFF, D)),
    }
    x = np.asarray(x, f)
    in_maps = []
    for c in range(8):
        b, s = divmod(c, NSEG)
        halo = np.zeros((2, D), f)
        if s > 0:
            halo[0] = x[b, s * T - 1]
        if s < NSEG - 1:
            halo[1] = x[b, (s + 1) * T]
        segw = np.zeros((1, 16), f)
        for r in range(NSEG):
            if r < s:
                segw[0, r] = T * (s - 1 - r)
                segw[0, 4 + r] = 1.0
            if r > s:
                segw[0, 8 + r] = T * (r - s - 1)
                segw[0, 12 + r] = 1.0
        m = dict(shared)
        m["x_seg"] = c_(x[b, s * T:(s + 1) * T])
        m["x_halo"] = halo
        m["pos"] = np.arange(s * T, (s + 1) * T, dtype=f).reshape(1, T)
        m["segw"] = segw
        in_maps.append(m)
    return in_maps


def kernel(x, g_mix, w_in, w_conv, dec_f, dec_b, g_ret, w_a_out, w_r_out, w_o,
           g_ffn, w_ff_gate, w_ff_up, w_ff_down, g_final):
    nc, dumps, kb, ar = build_program()
    in_maps = make_in_maps(x, g_mix, w_in, w_conv, dec_f, dec_b, g_ret, w_a_out, w_r_out, w_o,
                           g_ffn, w_ff_gate, w_ff_up, w_ff_down, g_final)
    res = run_bass_kernel_spmd(nc, in_maps, core_ids=list(range(8)))
    out = np.empty((2, SEQ, D), np.float32)
    for c in range(8):
        b, s = divmod(c, NSEG)
        out[b, s * T:(s + 1) * T] = np.asarray(res.results[c]["y"], np.float32)
    return out
```
